# Optimizing a Trainium2 kernel written in Bass

```python
import jax, jax.numpy as jnp
from jax import lax
import numpy as np

D_MODEL = 1024
BATCH = 8
SEQ = 8192
DEPTH = 2

NORM_EPS = 1e-6
PLE_DIM = 256
D_FF = 4 * D_MODEL
POOL_WINDOWS = (2, 4, 8, 16)
N_POOL_GROUPS = len(POOL_WINDOWS)
POOL_WIDTH = D_MODEL // 2
POOL_GROUP_DIM = POOL_WIDTH // N_POOL_GROUPS
HGRN_WIDTH = D_MODEL // 2
HGRN_HEAD_DIM = 128
HGRN_HEADS = HGRN_WIDTH // HGRN_HEAD_DIM
HGRN_CHUNK = 64
ATTN_HEADS = 16
ATTN_HEAD_DIM = D_MODEL // ATTN_HEADS
DILATION_PAIRS = ((128, 1), (512, 4), (2048, 16))
BLOCK_Q = 128
ROPE_THETA = 10000.0

kernel_name = 'hybrid_pool_hgrn2_dilated_attn_trunk'

F32 = jnp.float32


def _rms(x, g):
    xf = x.astype(F32)
    y = xf * lax.rsqrt(jnp.mean(xf * xf, axis=-1, keepdims=True) + NORM_EPS)
    return (y * g.astype(F32)).astype(x.dtype)


def _multiscale_pool(u, pool_w, pool_scale):
    B_, S_, _ = u.shape
    uf = u.astype(F32).reshape(B_, S_, N_POOL_GROUPS, POOL_GROUP_DIM)
    t = jnp.arange(S_)
    outs = []
    for gi, w in enumerate(POOL_WINDOWS):
        ug = uf[:, :, gi]
        cs = jnp.cumsum(ug, axis=1)
        lag = jnp.pad(cs[:, :S_ - w], ((0, 0), (w, 0), (0, 0)))
        cnt = jnp.minimum(t + 1, w).astype(F32)[None, :, None]
        outs.append((cs - lag) / cnt - ug)
    d = jnp.stack(outs, axis=2).astype(u.dtype)
    y = jnp.einsum('bsgc,gcd->bsgd', d, pool_w)
    return y.reshape(B_, S_, POOL_WIDTH) * pool_scale


def _hgrn2(q, f_pre, i_in, g, lb, o_gain):
    B_, S_, _ = q.shape
    H, K = HGRN_HEADS, HGRN_HEAD_DIM
    C = HGRN_CHUNK
    nC = S_ // C
    f = lb + (1.0 - lb) * jax.nn.sigmoid(f_pre.astype(F32))
    logf = jnp.log(f)
    k = 1.0 - f

    def chunks(a):
        return a.astype(F32).reshape(B_, nC, C, H, K).transpose(1, 0, 3, 2, 4)

    causal = jnp.tril(jnp.ones((C, C), dtype=bool))

    def step(state, inp):
        qc, kc, vc, gc = inp
        b = jnp.cumsum(gc, axis=2)
        o_inter = jnp.einsum('bhtk,bhkv->bhtv', qc * jnp.exp(b), state)
        diff = b[:, :, :, None, :] - b[:, :, None, :, :]
        decay = jnp.exp(jnp.where(causal[:, :, None], diff, -jnp.inf))
        attn = jnp.einsum('bhtk,bhtsk,bhsk->bhts', qc, decay, kc)
        o = o_inter + jnp.einsum('bhts,bhsv->bhtv', attn, vc)
        b_last = b[:, :, -1:, :]
        state = (jnp.exp(b_last[:, :, 0])[..., None] * state
                 + jnp.einsum('bhsk,bhsv->bhkv', kc * jnp.exp(b_last - b), vc))
        return state, o

    init = jnp.zeros((B_, H, K, K), F32)
    _, o = lax.scan(step, init, (chunks(q), chunks(k), chunks(i_in), chunks(logf)))
    o = o.transpose(1, 0, 3, 2, 4).reshape(B_, S_, H, K)
    o = o * lax.rsqrt(jnp.mean(o * o, axis=-1, keepdims=True) + NORM_EPS) * o_gain.astype(F32)
    o = o * jax.nn.silu(g.astype(F32).reshape(B_, S_, H, K))
    return o.reshape(B_, S_, HGRN_WIDTH)


def _rope(x, pos):
    half = x.shape[-1] // 2
    inv = ROPE_THETA ** (-jnp.arange(half, dtype=F32) / half)
    ang = pos[:, None] * inv[None, :]
    cos = jnp.cos(ang)[:, None, :]
    sin = jnp.sin(ang)[:, None, :]
    x1, x2 = x[..., :half], x[..., half:]
    return jnp.concatenate([x1 * cos - x2 * sin, x2 * cos + x1 * sin], axis=-1)


def _dilated_branch(q, k, v, d, w):
    B_, S_, H_, Dh = q.shape
    L = S_ // d

    def to_sub(a):
        return a.reshape(B_, L, d, H_, Dh).transpose(0, 2, 3, 1, 4)

    nb = -(-L // BLOCK_Q)
    Lq = nb * BLOCK_Q
    qs = jnp.pad(to_sub(q), ((0, 0), (0, 0), (0, 0), (0, Lq - L), (0, 0)))
    ks = jnp.pad(to_sub(k), ((0, 0), (0, 0), (0, 0), (w, Lq - L), (0, 0)))
    vs = jnp.pad(to_sub(v), ((0, 0), (0, 0), (0, 0), (w, Lq - L), (0, 0)))
    a_idx = jnp.arange(BLOCK_Q)[:, None]
    c_idx = jnp.arange(BLOCK_Q + w)[None, :]
    dist = a_idx - c_idx + w
    band = (dist >= 0) & (dist <= w)
    scale = Dh ** -0.5

    def block(n):
        start = n * BLOCK_Q
        qb = lax.dynamic_slice_in_dim(qs, start, BLOCK_Q, axis=3)
        kb = lax.dynamic_slice_in_dim(ks, start, BLOCK_Q + w, axis=3)
        vb = lax.dynamic_slice_in_dim(vs, start, BLOCK_Q + w, axis=3)
        valid = band & (start + c_idx - w >= 0)
        s = jnp.einsum('bdhqe,bdhke->bdhqk', qb, kb) * scale
        s = jnp.where(valid, s, -jnp.inf)
        m = jnp.max(s, axis=-1)
        pr = jnp.exp(s - m[..., None])
        l = jnp.sum(pr, axis=-1)
        acc = jnp.einsum('bdhqk,bdhke->bdhqe', pr, vb)
        return m, l, acc

    m, l, acc = lax.map(block, jnp.arange(nb))

    def from_sub(a):
        a = jnp.moveaxis(a, 0, 3)
        a = a.reshape(a.shape[:3] + (Lq,) + a.shape[5:])[:, :, :, :L]
        a = jnp.moveaxis(a, 3, 1)
        return a.reshape((B_, S_, H_) + a.shape[4:])

    return from_sub(m), from_sub(l), from_sub(acc)


def _dilated_attention(q, k, v):
    ms, ls, accs = [], [], []
    for window, dil in DILATION_PAIRS:
        m, l, acc = _dilated_branch(q, k, v, dil, window // dil)
        ms.append(m); ls.append(l); accs.append(acc)
    m_all = ms[0]
    for m in ms[1:]:
        m_all = jnp.maximum(m_all, m)
    num = jnp.zeros_like(accs[0])
    den = jnp.zeros_like(ls[0])
    for m, l, acc in zip(ms, ls, accs):
        wgt = jnp.exp(m - m_all)
        num = num + wgt[..., None] * acc
        den = den + wgt * l
    return num / den[..., None]


def setup_inputs(seed: int = 0) -> dict:
    key = jax.random.key(seed)
    ks = jax.random.split(key, 20)
    n_even = (DEPTH + 1) // 2
    n_odd = DEPTH // 2
    ab_in = POOL_WIDTH + 4 * HGRN_WIDTH
    ab_out = POOL_WIDTH + HGRN_WIDTH
    attn_w = ATTN_HEADS * ATTN_HEAD_DIM

    def nrm(k, shape, scale):
        return jax.random.normal(k, shape, F32) * scale

    def gain(k, shape):
        return 1.0 + 0.1 * jax.random.normal(k, shape, F32)

    return {
        'x': nrm(ks[0], (BATCH, SEQ, D_MODEL), 1.0),
        'p': nrm(ks[1], (DEPTH, BATCH, SEQ, PLE_DIM), 1.0),
        'mix_norm': gain(ks[2], (DEPTH, D_MODEL)),
        'w_in_ab': nrm(ks[3], (n_even, D_MODEL, ab_in), D_MODEL ** -0.5),
        'pool_w': nrm(ks[4], (n_even, N_POOL_GROUPS, POOL_GROUP_DIM, POOL_GROUP_DIM), POOL_GROUP_DIM ** -0.5),
        'pool_scale': gain(ks[5], (n_even, POOL_WIDTH)),
        'hgrn_lb': nrm(ks[6], (DEPTH + 1, HGRN_WIDTH), 0.5),
        'hgrn_o_norm': gain(ks[7], (n_even, HGRN_HEAD_DIM)),
        'w_out_ab': nrm(ks[8], (n_even, ab_out, D_MODEL), ab_out ** -0.5),
        'w_qkv': nrm(ks[9], (n_odd, D_MODEL, 3 * attn_w), D_MODEL ** -0.5),
        'q_norm': gain(ks[10], (n_odd, ATTN_HEAD_DIM)),
        'k_norm': gain(ks[11], (n_odd, ATTN_HEAD_DIM)),
        'w_o': nrm(ks[12], (n_odd, attn_w, D_MODEL), attn_w ** -0.5),
        'mlp_norm': gain(ks[13], (DEPTH, D_MODEL)),
        'w_up': nrm(ks[14], (DEPTH, D_MODEL, D_FF), D_MODEL ** -0.5),
        'w_down': nrm(ks[15], (DEPTH, D_FF, D_MODEL), D_FF ** -0.5),
        'ple_norm': gain(ks[16], (DEPTH, D_MODEL)),
        'w_ple': nrm(ks[17], (DEPTH, PLE_DIM, D_MODEL), PLE_DIM ** -0.5),
        'w_ple_gate': nrm(ks[18], (DEPTH, D_MODEL, D_MODEL), D_MODEL ** -0.5),
    }


def reference(x, p, mix_norm, w_in_ab, pool_w, pool_scale, hgrn_lb, hgrn_o_norm, w_out_ab,
              w_qkv, q_norm, k_norm, w_o, mlp_norm, w_up, w_down, ple_norm, w_ple, w_ple_gate):
    B_, S_, _ = x.shape
    pos = jnp.arange(S_, dtype=F32)
    lb_all = jnp.cumsum(jax.nn.softmax(hgrn_lb.astype(F32), axis=0), axis=0)
    h = x
    for layer in range(DEPTH):
        hn = _rms(h, mix_norm[layer])
        if layer % 2 == 0:
            e = layer // 2
            z = hn @ w_in_ab[e]
            o0 = POOL_WIDTH
            u = z[..., :o0]
            hq = z[..., o0:o0 + HGRN_WIDTH]
            hf = z[..., o0 + HGRN_WIDTH:o0 + 2 * HGRN_WIDTH]
            hi = z[..., o0 + 2 * HGRN_WIDTH:o0 + 3 * HGRN_WIDTH]
            hg = z[..., o0 + 3 * HGRN_WIDTH:]
            a_out = _multiscale_pool(u, pool_w[e], pool_scale[e])
            b_out = _hgrn2(hq, hf, hi, hg, lb_all[layer], hgrn_o_norm[e]).astype(h.dtype)
            mix = jnp.concatenate([a_out.astype(h.dtype), b_out], axis=-1) @ w_out_ab[e]
        else:
            ci = layer // 2
            z = (hn @ w_qkv[ci]).reshape(B_, S_, 3, ATTN_HEADS, ATTN_HEAD_DIM)
            q = _rope(_rms(z[:, :, 0], q_norm[ci]).astype(F32), pos)
            k = _rope(_rms(z[:, :, 1], k_norm[ci]).astype(F32), pos)
            v = z[:, :, 2].astype(F32)
            att = _dilated_attention(q, k, v).reshape(B_, S_, ATTN_HEADS * ATTN_HEAD_DIM)
            mix = att.astype(h.dtype) @ w_o[ci]
        h = h + mix
        hm = _rms(h, mlp_norm[layer])
        h = h + jnp.square(jax.nn.relu(hm @ w_up[layer])) @ w_down[layer]
        gate = jax.nn.sigmoid(_rms(h, ple_norm[layer]) @ w_ple_gate[layer])
        h = h + (p[layer].astype(h.dtype) @ w_ple[layer]) * gate
    return h
```

```python
import numpy as np
from contextlib import ExitStack
import concourse.bass as bass
import concourse.mybir as mybir
from concourse.bass_utils import run_bass_kernel_spmd

F32 = mybir.dt.float32
BF16 = mybir.dt.bfloat16
AF = mybir.ActivationFunctionType
ALU = mybir.AluOpType

D = 1024
SEQ = 8192
T = 512
NJ = 57
SLOT = 4096
EPS = 1e-6
ENGS = ("pe", "act", "dve", "pool", "sp")
DEF_COST = {"pe": 2.2, "act": 0.65, "dve": 0.7, "pool": 1.2, "sp": 4.0}
SYNC_LAT = 0.15
SCHED = True
DBG = 0


class Res:
    __slots__ = ("name", "w", "rs", "const", "excl")

    def __init__(self, name, const=False, excl=False):
        self.name = name
        self.w = None
        self.rs = []
        self.const = const
        self.excl = excl


class Op:
    __slots__ = ("eng", "fn", "deps", "marked", "count", "chan", "chan_count", "cost", "idx", "tag", "st", "ft")


class Prog:
    def __init__(self):
        self.ops = []
        self.by_eng = {e: [] for e in ENGS}
        self.chan_counts = {}

    def op(self, eng, fn, reads=(), writes=(), chan=None, cost=None):
        x = Op()
        x.eng = eng
        if cost is None:
            cost = 4.0 if chan is not None else DEF_COST[eng]
        x.cost = cost
        x.idx = len(self.ops)
        x.tag = getattr(self, "tag", "")
        x.fn = fn
        x.marked = False
        x.count = 0
        x.chan = chan
        x.chan_count = 0
        deps = []
        seen = set()

        def add(y):
            if y is not None and id(y) not in seen:
                seen.add(id(y))
                deps.append(y)

        writes = list(writes) + [r for r in reads if r.excl]
        reads = [r for r in reads if not r.excl]
        for r in reads:
            add(r.w)
        for w in writes:
            add(w.w)
            for y in w.rs:
                add(y)
        for r in reads:
            if not r.const:
                r.rs.append(x)
        for w in writes:
            w.w = x
            w.rs = []
        x.deps = deps
        self.ops.append(x)
        return x

    def schedule(self):
        import heapq
        ops = self.ops
        n = len(ops)
        succ = [[] for _ in range(n)]
        indeg = [0] * n
        for x in ops:
            indeg[x.idx] = len(x.deps)
            for y in x.deps:
                succ[y.idx].append(x.idx)
        ready_t = [0.0] * n
        fin = [0.0] * n
        free = {e: 0.0 for e in ENGS}
        pend = {e: [] for e in ENGS}
        avail = {e: [] for e in ENGS}
        for x in ops:
            if indeg[x.idx] == 0:
                heapq.heappush(pend[x.eng], (0.0, x.idx))
        order = {e: [] for e in ENGS}
        done = 0
        while done < n:
            best = None
            for e in ENGS:
                pe_, av = pend[e], avail[e]
                while pe_ and pe_[0][0] <= free[e]:
                    heapq.heappush(av, heapq.heappop(pe_)[1])
                if av:
                    cand = (free[e], av[0], e, True)
                elif pe_:
                    cand = (pe_[0][0], pe_[0][1], e, False)
                else:
                    continue
                if best is None or cand[:2] < best[:2]:
                    best = cand
            st, i, e, from_av = best
            if from_av:
                heapq.heappop(avail[e])
            else:
                heapq.heappop(pend[e])
            x = ops[i]
            if x.chan is not None:
                free[e] = st + 0.15
                fin[i] = st + x.cost
            else:
                free[e] = st + x.cost
                fin[i] = free[e]
            order[e].append(x)
            x.st, x.ft = st, fin[i]
            done += 1
            for j in succ[i]:
                if fin[i] > ready_t[j]:
                    ready_t[j] = fin[i]
                indeg[j] -= 1
                if indeg[j] == 0:
                    heapq.heappush(pend[ops[j].eng], (ready_t[j] + SYNC_LAT, j))
        self.by_eng = order
        self.sim_time = max(fin) if n else 0.0

    def assign_chans(self):
        self.chan_counts = {}
        allops = []
        for e in ENGS:
            allops.extend(self.by_eng[e])
        for e in ENGS:
            for x in self.by_eng[e]:
                if x.chan is not None:
                    self.chan_counts[x.chan] = self.chan_counts.get(x.chan, 0) + 16
                    x.chan_count = self.chan_counts[x.chan]

    def finalize(self):
        for x in self.ops:
            for y in x.deps:
                if y.chan is not None:
                    continue
                if y.eng == "pe" and x.eng == "pe":
                    continue
                y.marked = True
        for e in ENGS:
            c = 0
            for x in self.by_eng[e]:
                if x.marked:
                    c += 1
                    x.count = c

    def emit(self, block, sems, chan_sems, final_waits=()):
        self.finalize()
        self.assign_chans()
        engs = {"pe": block.tensor, "act": block.scalar, "dve": block.vector,
                "pool": block.gpsimd, "sp": block.sync}
        prog = self

        def make(ename):
            def body(e):
                waited = {}
                for x in prog.by_eng[ename]:
                    for y in x.deps:
                        if y.chan is not None:
                            key, val, sem = "c:" + y.chan, y.chan_count, chan_sems[y.chan]
                        else:
                            if y.eng == "pe" and ename == "pe":
                                continue
                            key, val, sem = y.eng, y.count, sems[y.eng]
                        if waited.get(key, 0) < val:
                            e.wait_ge(sem, val)
                            waited[key] = val
                    ins = x.fn(e)
                    if x.chan is not None:
                        ins.then_inc(chan_sems[x.chan], 16)
                    elif x.marked:
                        ins.then_inc(sems[ename], 1)
                if ename == "sp":
                    for ch in final_waits:
                        e.wait_ge(chan_sems[ch], prog.chan_counts[ch])
            return body

        for ename in ENGS:
            engs[ename](make(ename))


def _slotify(w, rows, c0, nc_):
    out = np.empty((128, len(rows), nc_), np.float32)
    for k, r0 in enumerate(rows):
        out[:, k, :] = w[r0:r0 + 128, c0:c0 + nc_]
    return out.reshape(128, len(rows) * nc_)


def _pack_weights(inp):
    ws = np.zeros((NJ, 128, SLOT), np.float32)
    r8 = [k * 128 for k in range(8)]
    w_in = inp["w_in_ab"][0]
    ws[0] = _slotify(w_in, r8, 0, 512)
    for hh in range(4):
        t = np.empty((128, 8, 512), np.float32)
        for k in range(8):
            rows = slice(k * 128, k * 128 + 128)
            t[:, k, 0:128] = w_in[rows, 512 + hh * 128:512 + hh * 128 + 128]
            t[:, k, 128:256] = w_in[rows, 1024 + hh * 128:1024 + hh * 128 + 128]
            t[:, k, 256:384] = w_in[rows, 2048 + hh * 128:2048 + hh * 128 + 128]
            t[:, k, 384:512] = w_in[rows, 1536 + hh * 128:1536 + hh * 128 + 128]
        ws[1 + hh] = t.reshape(128, SLOT)
    w_out = inp["w_out_ab"][0]
    ws[5] = _slotify(w_out, r8, 0, 512)
    ws[6] = _slotify(w_out, r8, 512, 512)

    def mlp_ple(base, l):
        wu, wd = inp["w_up"][l], inp["w_down"][l]
        j = base
        for qd in range(4):
            for hf in range(2):
                ws[j] = _slotify(wu, r8, qd * 1024 + hf * 512, 512)
                j += 1
            for nh in range(2):
                ws[j] = _slotify(wd, [qd * 1024 + k * 128 for k in range(8)], nh * 512, 512)
                j += 1
        wp, wg = inp["w_ple"][l], inp["w_ple_gate"][l]
        for nh in range(2):
            ws[j][:, 0:1024] = _slotify(wp, [0, 128], nh * 512, 512)
            j += 1
            ws[j] = _slotify(wg, r8, nh * 512, 512)
            j += 1
        return j

    j = mlp_ple(7, 0)
    assert j == 27
    wqkv = inp["w_qkv"][0]
    for g in range(4):
        t = np.empty((128, 8, 512), np.float32)
        for k in range(8):
            rows = slice(k * 128, k * 128 + 128)
            t[:, k, 0:256] = wqkv[rows, 256 * g:256 * g + 256]
            t[:, k, 256:512] = wqkv[rows, 1024 + 256 * g:1024 + 256 * g + 256]
        ws[27 + 2 * g] = t.reshape(128, SLOT)
        ws[28 + 2 * g][:, 0:2048] = _slotify(wqkv, r8, 2048 + 256 * g, 256)
    wo = inp["w_o"][0]
    ws[35] = _slotify(wo, r8, 0, 512)
    ws[36] = _slotify(wo, r8, 512, 512)
    j = mlp_ple(37, 1)
    assert j == NJ
    return ws


def _col8(v):
    return np.ascontiguousarray(v.reshape(8, 128).T)


def _pack_vec(inp):
    cols = []
    for nm, l in (("mix_norm", 0), ("mlp_norm", 0), ("ple_norm", 0),
                  ("mix_norm", 1), ("mlp_norm", 1), ("ple_norm", 1)):
        cols.append(_col8(inp[nm][l]))
    cols.append(np.ascontiguousarray(inp["pool_scale"][0].reshape(4, 128).T))
    lb = inp["hgrn_lb"]
    cols.append(np.ascontiguousarray(lb.reshape(3, 4, 128).transpose(2, 0, 1).reshape(128, 12)))
    cols.append(inp["hgrn_o_norm"][0].reshape(128, 1))
    idx = np.arange(128) % 64
    sw = (idx + 32) % 64
    qn, kn = inp["q_norm"][0], inp["k_norm"][0]
    cols.append(np.stack([qn[idx], qn[sw], kn[idx], kn[sw]], axis=1))
    return np.ascontiguousarray(np.concatenate(cols, axis=1).astype(np.float32))


NV = 69
NCB = 11 * 128
NCF = 512 + 64 + 1


def _consts(seq):
    a = np.arange(128)
    kp, qi = a[:, None], a[None, :]
    ident = (kp == qi)
    ones = np.ones((128, 128))
    bd64 = (kp // 64 == qi // 64)
    prot = np.zeros((128, 128))
    for m in range(128):
        if m % 64 < 32:
            prot[m + 32, m] = -1.0
        else:
            prot[m - 32, m] = 1.0
    hmask = (kp <= qi) & (kp // 64 == qi // 64)
    m1p = kp >= qi
    m1c = kp <= qi
    same = ((kp - qi) % 4 == 0)
    m16f = same & (kp >= qi)
    m16m = same
    m16c = same & (kp <= qi)
    mpp = m1p.astype(np.float32) + m16m
    mpc = m1c.astype(np.float32) + m16c
    cbf = np.concatenate([np.asarray(x, np.float32) for x in
                          (ident, ones, bd64, prot, hmask, m1p, m1c, m16f, m16m, mpp, mpc)], axis=1)
    scanm = np.ones((128, 512), np.float32)
    scanm[:, ::64] = 0.0
    invc = np.zeros((128, 64), np.float32)
    for g, w in enumerate((2, 4, 8, 16)):
        invc[:, g * 16:(g + 1) * 16] = 1.0 / np.minimum(np.arange(16) + 1, w)
    cf = np.concatenate([scanm, invc, np.full((128, 1), EPS, np.float32)], axis=1)
    half = 32
    inv = (10000.0 ** (-np.arange(half, dtype=np.float32) / half)).astype(np.float32)
    pos = np.arange(seq, dtype=np.float32)
    ang = (pos[None, :] * inv[(np.arange(128) % 64) % 32][:, None]).astype(np.float32)
    return (np.ascontiguousarray(cbf.astype(np.float32)), np.ascontiguousarray(cf),
            np.cos(ang).astype(np.float32), np.sin(ang).astype(np.float32))


def build(nt, upto=6):
    S = nt * T
    nc = bass.Bass("TRN2", target_bir_lowering=False)
    P = Prog()
    es = ExitStack()

    def dram(name, shape, dt, kind):
        return nc.dram_tensor(name, shape, dt, kind=kind).ap()

    global LAST_PROG
    LAST_PROG = P
    x_d = dram("x", [S, D], F32, "ExternalInput")
    p_d = dram("p", [2, S, 256], F32, "ExternalInput")
    wsrc = dram("wsrc", [NJ, 128, SLOT], F32, "ExternalInput")
    vec_d = dram("vec", [128, NV], F32, "ExternalInput")
    cbf_d = dram("cbf", [128, NCB], F32, "ExternalInput")
    cf_d = dram("cf", [128, NCF], F32, "ExternalInput")
    cos_d = dram("cosd", [128, S], F32, "ExternalInput")
    sin_d = dram("sind", [128, S], F32, "ExternalInput")
    out_d = dram("out", [S, D], F32, "ExternalOutput")
    wsc = dram("wsc", [NJ, 128, SLOT], BF16, "Internal")

    def sb(name, shape, dt):
        return es.enter_context(nc.sbuf_tensor(name, shape, dt))

    h = sb("h", [128, 4, D], F32)
    xtok2 = [sb(f"xtok{q}", [128, D], BF16) for q in range(2)]
    xnT = sb("xnT", [128, 8, T], BF16)
    abT = sb("abT", [128, 8, T], BF16)
    arena = sb("arena", [128, 12, 528], F32)
    wslot = [sb(f"wslot{i}", [128, SLOT], BF16) for i in range(3)]
    Kperm = sb("Kperm", [128, 8, 5 * T], BF16)
    Vperm = sb("Vperm", [128, 20, D], BF16)
    Knat = sb("Knat", [128, 8, 128 + T], BF16)
    Vnat = sb("Vnat", [128, 5, D], BF16)
    cosb = sb("cosb", [128, T], F32)
    sinb = sb("sinb", [128, T], F32)
    ptok = sb("ptok", [128, 4, 256], BF16)
    pT = sb("pT", [128, 2, T], BF16)
    cbf = sb("cbf_sb", [128, NCB], BF16)
    cf = sb("cf_sb", [128, NCF], F32)
    vec = sb("vecs", [128, NV], F32)
    S32 = sb("S32", [128, 4, 128], F32)
    uh = sb("uh", [128, 4, 16], F32)
    stat = sb("stat", [128, 32], F32)
    lbt = sb("lbt", [128, 24], F32)
    Dd2 = sb("Dd2", [128, 2, 8], F32)
    itok2 = sb("itok2", [128, 2, 512], BF16)
    tot = sb("tot", [128, 4, 128], F32)
    psb = [es.enter_context(nc.psum_tensor(f"ps{i}", [128, 512], F32)) for i in range(8)]

    R_h = [Res(f"h{b}") for b in range(4)]
    R_xtok2 = [Res("xtok0"), Res("xtok1")]
    R_xnT = Res("xnT")
    R_ab = [Res(f"ab{c}") for c in range(8)]
    R_ar = [Res(f"ar{r}") for r in range(12)]
    R_ws = [Res(f"ws{i}") for i in range(3)]
    R_wsc = [Res(f"wsc{j}") for j in range(NJ)]
    R_prep = [Res(f"prep{j}") for j in range(4)]
    R_KP = [[Res(f"kp{s}_{g}") for g in range(4)] for s in range(5)]
    R_VP = [[Res(f"vp{s}_{g}") for g in range(4)] for s in range(5)]
    R_KN = [Res(f"kn{g}") for g in range(4)]
    R_VN = [Res(f"vn{g}") for g in range(4)]
    R_rope = Res("rope")
    R_ptok = Res("ptok")
    R_pT = Res("pT")
    R_c = Res("consts", const=True)
    R_S = [Res(f"S{hh}") for hh in range(4)]
    R_uh = Res("uh")
    R_stat4 = [Res(f"stat{b}") for b in range(4)]
    R_lb = Res("lb", const=True)
    R_Dd2 = [Res("Dd0"), Res("Dd1")]
    R_itok2 = [Res("itok0"), Res("itok1")]
    R_tot = Res("tot")
    R_pb = [Res(f"pb{i}", excl=True) for i in range(8)]

    def cb(i):
        return cbf[:, i * 128:(i + 1) * 128]

    ident, ones_b, bd64, prot, hmask = cb(0), cb(1), cb(2), cb(3), cb(4)
    m1p, m1c, m16f, m16m, mpp, mpc = cb(5), cb(6), cb(7), cb(8), cb(9), cb(10)
    scanm = cf[:, 0:512]
    eps_ap = cf[:, 576:577]

    def vcol(i):
        return vec[:, i:i + 1]

    pstate = {"next": 0, "unread": [False] * 8}

    def pb():
        i = pstate["next"]
        pstate["next"] = (i + 1) % 6
        assert not pstate["unread"][i], f"psum bank {i} reallocated before read"
        pstate["unread"][i] = True
        return i

    def pread(i):
        pstate["unread"][i] = False

    def ps_bf(i):
        return psb[i][:].bitcast(BF16)

    R_c = Res("consts", const=True)
    P.op("pool", lambda e: e.dma_start(out=cbf[:], in_=cbf_d[:, :]), writes=[R_c], chan="constp")
    P.op("sp", lambda e: e.dma_start(out=cf[:], in_=cf_d[:, :]), reads=[R_c], writes=[R_c], chan="const")
    P.op("sp", lambda e: e.dma_start(out=vec[:], in_=vec_d[:, :]), reads=[R_c], writes=[R_c], chan="const")
    pw_sb = sb("pw_sb", [128, 4, 128], BF16)
    pool_w_d = dram("pool_w", [4, 128, 128], F32, "ExternalInput")
    P.op("pool", lambda e: e.dma_start(out=pw_sb[:], in_=pool_w_d.rearrange("g c d -> c g d")), reads=[R_c], writes=[R_c], chan="constp")
    P.op("dve", lambda e: e.memset(S32[:], 0.0), writes=R_S)
    P.op("dve", lambda e: e.memset(uh[:], 0.0), writes=[R_uh])
    P.op("act", lambda e: e.activation(lbt[:, 0:12], vec[:, 52:64], AF.Exp), reads=[R_c], writes=[R_lb])
    P.op("dve", lambda e: e.tensor_tensor(lbt[:, 20:24], lbt[:, 0:4], lbt[:, 4:8], ALU.add), reads=[R_lb], writes=[R_lb])
    P.op("dve", lambda e: e.tensor_tensor(lbt[:, 20:24], lbt[:, 20:24], lbt[:, 8:12], ALU.add), reads=[R_lb], writes=[R_lb])
    P.op("dve", lambda e: e.reciprocal(lbt[:, 20:24], lbt[:, 20:24]), reads=[R_lb], writes=[R_lb])
    P.op("dve", lambda e: e.tensor_tensor(lbt[:, 12:16], lbt[:, 0:4], lbt[:, 20:24], ALU.mult), reads=[R_lb], writes=[R_lb])
    P.op("dve", lambda e: e.tensor_scalar(lbt[:, 16:20], lbt[:, 12:16], -1.0, 1.0, ALU.mult, ALU.add), reads=[R_lb], writes=[R_lb])

    gain_of = {}
    for j in range(0, 5):
        gain_of[j] = 0
    for base, l in ((7, 0), (37, 1)):
        for qd in range(4):
            gain_of[base + 4 * qd] = 1 + 3 * l
            gain_of[base + 4 * qd + 1] = 1 + 3 * l
        gain_of[base + 17] = 2 + 3 * l
        gain_of[base + 19] = 2 + 3 * l
    for j in range(27, 35):
        gain_of[j] = 3
    nel = {j: SLOT for j in range(NJ)}
    for j in (23, 25, 53, 55):
        nel[j] = 1024
    for j in (28, 30, 32, 34):
        nel[j] = 2048
    stg_state = {"n": 0}
    R_thr = [Res(f"thr{j}") for j in range(NJ)]
    PREP_AHEAD = 6

    def emit_prep(j):
        n = nel[j]
        thr = [R_thr[j - PREP_AHEAD]] if j >= PREP_AHEAD else []
        if j in gain_of:
            stg_n = stg_state["n"]
            s = 1 + stg_n % 3
            use_v = (stg_n // 3) % 2 == 1
            stg_state["n"] += 1
            ncol = n // 8
            if use_v:
                stg = Vperm[:, 4 * s:4 * s + 4, :].rearrange("p a (b n) -> p (a b) n", b=2)[:, :, 0:ncol]
                stg_res = R_VP[s]
            else:
                stg = Kperm[:, :, s * T:s * T + ncol]
                stg_res = R_KP[s]
            tagc = ("v" if use_v else "k") + str(s)
            P.op("pool", lambda e, j=j, stg=stg, n=n: e.dma_start(out=stg, in_=wsrc[j, :, 0:n].rearrange("p (k n) -> p k n", k=8)),
                 reads=thr, writes=stg_res, chan=f"pw{tagc}")
            gi = gain_of[j]
            P.op("dve", lambda e, stg=stg, gi=gi, ncol=ncol: e.tensor_tensor(
                stg, stg, vec[:, gi * 8:gi * 8 + 8].unsqueeze(2).to_broadcast([128, 8, ncol]), ALU.mult),
                reads=[R_c] + stg_res, writes=stg_res, cost=2.4)
            P.op("sp", lambda e, j=j, stg=stg, n=n: e.dma_start(out=wsc[j, :, 0:n].rearrange("p (k n) -> p k n", k=8), in_=stg),
                 reads=stg_res, writes=[R_wsc[j]], chan=f"ps{tagc}")
        else:
            P.op("pool", lambda e, j=j, n=n: e.dma_start(out=wsc[j, :, 0:n], in_=wsrc[j, :, 0:n]),
                 reads=thr, writes=[R_wsc[j], R_prep[j % 4]], chan=f"prep{j % 4}")

    for j in range(min(PREP_AHEAD, NJ)):
        emit_prep(j)

    wst = {"loaded": -1}

    def wuse(i, k_first, k_last=None):
        if k_last is None:
            k_last = k_first
        n_first = i * NJ + k_first
        n_last = i * NJ + k_last
        assert n_last <= n_first + 2
        lim = min(n_first + 2, nt * NJ - 1)
        while wst["loaded"] < lim:
            n = wst["loaded"] + 1
            jt, s = n % NJ, n % 3
            ne = nel[jt]
            P.op("sp", lambda e, jt=jt, s=s, ne=ne: e.dma_start(out=wslot[s][:, 0:ne], in_=wsc[jt, :, 0:ne]),
                 reads=[R_wsc[jt]], writes=[R_ws[s]] + ([R_thr[n]] if n < NJ else []), chan=f"w{s}")
            wst["loaded"] = n
            if n + PREP_AHEAD < NJ:
                emit_prep(n + PREP_AHEAD)
        return [((i * NJ + k) % 3) for k in range(k_first, k_last + 1)]

    def w3(s, ncol=512):
        return wslot[s][:, 0:8 * ncol].rearrange("p (k n) -> p k n", k=8)

    def norm_stage(i):
        for b in range(4):
            xtok = xtok2[b % 2]
            R_xtok = R_xtok2[b % 2]
            R_stat = R_stat4[b]
            P.op("pool", lambda e, b=b: e.memset(stat[:, b:b + 1], 0.0), writes=[R_stat], cost=0.1)
            P.op("act", lambda e, b=b, xtok=xtok: e.activation(xtok[:], h[:, b, :], AF.Square, accum_out=stat[:, b:b + 1]),
                 reads=[R_h[b], R_stat], writes=[R_xtok, R_stat], cost=1.1)
            P.op("act", lambda e, b=b: e.activation(stat[:, 8 + b:9 + b], stat[:, b:b + 1], AF.Ln, bias=eps_ap, scale=1.0 / D),
                 reads=[R_stat, R_c], writes=[R_stat], cost=0.3)
            P.op("act", lambda e, b=b: e.activation(stat[:, 16 + b:17 + b], stat[:, 8 + b:9 + b], AF.Exp, scale=-0.5),
                 reads=[R_stat], writes=[R_stat], cost=0.3)
            P.op("dve", lambda e, b=b, xtok=xtok: e.tensor_scalar(xtok[:], h[:, b, :], stat[:, 16 + b:17 + b], None, ALU.mult),
                 reads=[R_h[b], R_stat, R_xtok], writes=[R_xtok], cost=1.2)
            k = pb()

            def tr(e, k=k, xtok=xtok):
                ins = None
                for c in range(8):
                    ins = e.transpose(ps_bf(k)[:, c * 128:(c + 1) * 128], xtok[:, c * 128:(c + 1) * 128], ident)
                return ins
            P.op("pe", tr, reads=[R_xtok, R_c], writes=[R_pb[k]], cost=1.0)
            P.op("act", lambda e, k=k, b=b: e.activation(
                xnT[:, :, b * 128:(b + 1) * 128], ps_bf(k).rearrange("p (c t) -> p c t", c=8), AF.Copy),
                reads=[R_pb[k]], writes=[R_xnT], cost=1.05)
            pread(k)

    def tokmajor_proj(i, jk, src, src_res, nh, first_dst=None):
        (s,) = wuse(i, jk)
        for b in range(4):
            k = pb()

            def mm(e, k=k, b=b, s=s):
                ins = None
                for kc in range(8):
                    ins = e.matmul(psb[k][:, :], src[:, kc, b * 128:(b + 1) * 128], w3(s)[:, kc, :],
                                   start=(kc == 0), stop=(kc == 7))
                return ins
            P.op("pe", mm, reads=list(src_res) + [R_ws[s]], writes=[R_pb[k]])
            P.op("dve", lambda e, k=k, b=b, nh=nh: e.tensor_tensor(
                h[:, b, nh * 512:(nh + 1) * 512], h[:, b, nh * 512:(nh + 1) * 512], psb[k][:, :], ALU.add),
                reads=[R_pb[k], R_h[b]], writes=[R_h[b]])
            pread(k)

    def mlp_stage(i, base):
        norm_stage(i)
        hid = [arena[:, 0:4, :].bitcast(BF16), arena[:, 4:8, :].bitcast(BF16)]
        hid_res = [R_ar[0:4], R_ar[4:8]]

        def hv(par, c):
            return hid[par][:, c // 2, (c % 2) * 512:(c % 2) * 512 + 512]

        for qd in range(4):
            par = qd % 2
            for hf in range(2):
                (s,) = wuse(i, base + 4 * qd + hf)
                for m in range(4):
                    k = pb()
                    c = hf * 4 + m

                    def mm(e, k=k, m=m, s=s):
                        ins = None
                        for kc in range(8):
                            ins = e.matmul(psb[k][:, :], w3(s)[:, kc, m * 128:(m + 1) * 128], xnT[:, kc, :],
                                           start=(kc == 0), stop=(kc == 7))
                        return ins
                    P.op("pe", mm, reads=[R_xnT, R_ws[s]], writes=[R_pb[k]])
                    rr = hid_res[par][c // 2]
                    P.op("act", lambda e, k=k, par=par, c=c: e.activation(hv(par, c), psb[k][:, :], AF.Relu),
                         reads=[R_pb[k]], writes=[rr])
                    pread(k)
                    P.op("pool", lambda e, par=par, c=c: e.tensor_tensor(hv(par, c), hv(par, c), hv(par, c), ALU.mult),
                         reads=[rr], writes=[rr], cost=1.1)
            for nh in range(2):
                (s,) = wuse(i, base + 4 * qd + 2 + nh)
                for b in range(4):
                    k = pb()

                    def mm(e, k=k, b=b, s=s, par=par):
                        ins = None
                        for kc in range(8):
                            ins = e.matmul(psb[k][:, :], hv(par, kc)[:, b * 128:(b + 1) * 128], w3(s)[:, kc, :],
                                           start=(kc == 0), stop=(kc == 7))
                        return ins
                    P.op("pe", mm, reads=hid_res[par] + [R_ws[s]], writes=[R_pb[k]])
                    P.op("dve", lambda e, k=k, b=b, nh=nh: e.tensor_tensor(
                        h[:, b, nh * 512:(nh + 1) * 512], h[:, b, nh * 512:(nh + 1) * 512], psb[k][:, :], ALU.add),
                        reads=[R_pb[k], R_h[b]], writes=[R_h[b]])
                    pread(k)

    def ple_stage(i, base, l):
        norm_stage(i)
        P.op("pool", lambda e: e.dma_start(
            out=ptok[:], in_=p_d[l, i * T:(i + 1) * T, :].rearrange("(b p) f -> p b f", p=128)),
            writes=[R_ptok], chan="pld")
        k = pb()

        def tr(e, k=k):
            ins = None
            for pc in range(2):
                for b in range(4):
                    ins = e.transpose(ps_bf(k)[:, pc * 512 + b * 128:pc * 512 + (b + 1) * 128],
                                      ptok[:, b, pc * 128:(pc + 1) * 128], ident)
            return ins
        P.op("pe", tr, reads=[R_ptok, R_c], writes=[R_pb[k]])
        P.op("act", lambda e, k=k: e.activation(pT[:].rearrange("p c t -> p (c t)"), ps_bf(k), AF.Copy),
             reads=[R_pb[k]], writes=[R_pT])
        pread(k)
        for nh in range(2):
            sp_, sg = wuse(i, base + 2 * nh, base + 2 * nh + 1)
            for b in range(4):
                par = (nh * 4 + b) % 2
                gt = arena[:, 2 * par, 0:512]
                pw = arena[:, 2 * par + 1, 0:512]
                rg, rp = R_ar[2 * par], R_ar[2 * par + 1]
                kg = pb()

                def mmg(e, kg=kg, b=b, sg=sg):
                    ins = None
                    for kc in range(8):
                        ins = e.matmul(psb[kg][:, :], xnT[:, kc, b * 128:(b + 1) * 128], w3(sg)[:, kc, :],
                                       start=(kc == 0), stop=(kc == 7))
                    return ins
                P.op("pe", mmg, reads=[R_xnT, R_ws[sg]], writes=[R_pb[kg]])
                kp_ = pb()

                def mmp(e, kp_=kp_, b=b, sp_=sp_):
                    ins = None
                    wv = wslot[sp_][:, 0:1024].rearrange("p (k n) -> p k n", k=2)
                    for pc in range(2):
                        ins = e.matmul(psb[kp_][:, :], pT[:, pc, b * 128:(b + 1) * 128], wv[:, pc, :],
                                       start=(pc == 0), stop=(pc == 1))
                    return ins
                P.op("pe", mmp, reads=[R_pT, R_ws[sp_]], writes=[R_pb[kp_]])
                P.op("act", lambda e, kg=kg, gt=gt: e.activation(gt, psb[kg][:, :], AF.Sigmoid),
                     reads=[R_pb[kg]], writes=[rg])
                pread(kg)
                P.op("dve", lambda e, kp_=kp_, gt=gt, pw=pw: e.tensor_tensor(pw, psb[kp_][:, :], gt, ALU.mult),
                     reads=[R_pb[kp_], rg], writes=[rp])
                pread(kp_)
                P.op("dve", lambda e, b=b, nh=nh, pw=pw: e.tensor_tensor(
                    h[:, b, nh * 512:(nh + 1) * 512], h[:, b, nh * 512:(nh + 1) * 512], pw, ALU.add),
                    reads=[rp, R_h[b]], writes=[R_h[b]])

    def l0_mixer(i):
        norm_stage(i)
        if DBG == 1:
            return
        (s,) = wuse(i, 0)
        for g in range(4):
            ub = arena[:, 5, :]
            tA = arena[:, 6, :]
            tB = arena[:, 7, :]
            dd = arena[:, 9, :].bitcast(BF16)[:, 0:512]
            k = pb()

            def mm(e, k=k, g=g, s=s):
                ins = None
                for kc in range(8):
                    ins = e.matmul(psb[k][:, :], w3(s)[:, kc, g * 128:(g + 1) * 128], xnT[:, kc, :],
                                   start=(kc == 0), stop=(kc == 7))
                return ins
            P.op("pe", mm, reads=[R_xnT, R_ws[s]], writes=[R_pb[k]])
            P.op("dve", lambda e, g=g: e.tensor_copy(ub[:, 0:16], uh[:, g, :]), reads=[R_uh], writes=[R_ar[5]])
            P.op("act", lambda e, k=k: e.activation(ub[:, 16:528], psb[k][:, :], AF.Copy),
                 reads=[R_pb[k]], writes=[R_ar[5]])
            pread(k)
            P.op("dve", lambda e, g=g: e.tensor_copy(uh[:, g, :], ub[:, 512:528]), reads=[R_ar[5]], writes=[R_uh])
            src, src_r = ub, R_ar[5]
            bufs = [(tA, R_ar[6]), (tB, R_ar[7])]
            sh = 1
            for step in range(g + 1):
                dst, dst_r = bufs[step % 2]
                lo = 2 * sh - 1
                P.op("dve", lambda e, src=src, dst=dst, sh=sh, lo=lo: e.tensor_tensor(
                    dst[:, lo:528], src[:, lo:528], src[:, lo - sh:528 - sh], ALU.add),
                    reads=[src_r], writes=[dst_r])
                src, src_r = dst, dst_r
                sh *= 2
            w = 2 ** (g + 1)
            P.op("dve", lambda e, src=src, w=w: e.scalar_tensor_tensor(
                dd, src[:, 16:528], 1.0 / w, ub[:, 16:528], ALU.mult, ALU.subtract),
                reads=[src_r, R_ar[5]], writes=[R_ar[9]])
            if i == 0:
                P.op("dve", lambda e, src=src, g=g: e.tensor_tensor(
                    src[:, 16:32], src[:, 16:32], cf[:, 512 + g * 16:512 + (g + 1) * 16], ALU.mult),
                    reads=[src_r, R_c], writes=[src_r])
                P.op("dve", lambda e, src=src: e.tensor_tensor(dd[:, 0:16], src[:, 16:32], ub[:, 16:32], ALU.subtract),
                     reads=[src_r, R_ar[5], R_ar[9]], writes=[R_ar[9]])
            k2 = pb()
            P.op("pe", lambda e, k2=k2, g=g: e.matmul(psb[k2][:, :], pw_sb[:, g, :], dd, start=True, stop=True),
                 reads=[R_ar[9], R_c], writes=[R_pb[k2]])
            P.op("act", lambda e, k2=k2, g=g: e.activation(abT[:, g, :], psb[k2][:, :], AF.Copy, scale=vcol(48 + g)),
                 reads=[R_pb[k2], R_c], writes=[R_ab[g]])
            pread(k2)
        if DBG == 2:
            return
        A0 = arena[:, 0, 0:512]
        A1 = arena[:, 1, 0:512]
        A2 = arena[:, 2, 0:512]
        A3 = arena[:, 3, 0:512]
        A4s = [(arena[:, 8, 0:512], R_ar[8]), (arena[:, 10, 0:512], R_ar[10])]
        QKs = [(arena[:, 4, :].bitcast(BF16), R_ar[4]), (arena[:, 11, :].bitcast(BF16), R_ar[11])]
        r5 = arena[:, 5, :].bitcast(BF16)
        r6 = arena[:, 6, :].bitcast(BF16)
        r7 = arena[:, 7, :].bitcast(BF16)
        T9 = arena[:, 9, 0:512]
        khT = r5[:, 512:1024].rearrange("p (b v) -> p b v", b=4)
        AT = r6[:, 0:512].rearrange("p (b v) -> p b v", b=4)
        osq = r6[:, 512:1024]
        Sdb = r7[:, 0:1024].rearrange("p (c v) -> p c v", c=8)

        def early(hh):
            par = hh % 2
            A4, rA4 = A4s[par]
            QK, rQK = QKs[par]
            qt, kh = QK[:, 0:512], QK[:, 512:1024]
            Ddp = Dd2[:, par, :]
            rDd = R_Dd2[par]
            (s,) = wuse(i, 1 + hh)
            kq, kf, kg = pb(), pb(), pb()
            for kk, off in ((kq, 0), (kf, 128), (kg, 256)):
                def mm(e, kk=kk, off=off, s=s):
                    ins = None
                    for kc in range(8):
                        ins = e.matmul(psb[kk][:, :], w3(s)[:, kc, off:off + 128], xnT[:, kc, :],
                                       start=(kc == 0), stop=(kc == 7))
                    return ins
                P.op("pe", mm, reads=[R_xnT, R_ws[s]], writes=[R_pb[kk]])
            ki = pb()

            def mmi(e, ki=ki, s=s):
                ins = None
                for b in range(4):
                    for kc in range(8):
                        ins = e.matmul(psb[ki][:, b * 128:(b + 1) * 128], xnT[:, kc, b * 128:(b + 1) * 128],
                                       w3(s)[:, kc, 384:512], start=(kc == 0), stop=(kc == 7))
                return ins
            P.op("pe", mmi, reads=[R_xnT, R_ws[s]], writes=[R_pb[ki]])
            P.op("act", lambda e, kf=kf: e.activation(A0, psb[kf][:, :], AF.Sigmoid),
                 reads=[R_pb[kf]], writes=[R_ar[0]])
            pread(kf)
            P.op("act", lambda e, kg=kg, A4=A4: e.activation(A4, psb[kg][:, :], AF.Sigmoid),
                 reads=[R_pb[kg]], writes=[rA4])
            P.op("act", lambda e, ki=ki, par=par: e.activation(itok2[:, par, :], psb[ki][:, :], AF.Copy),
                 reads=[R_pb[ki]], writes=[R_itok2[par]])
            pread(ki)
            P.op("dve", lambda e, kg=kg, A4=A4: e.scalar_tensor_tensor(A4, psb[kg][:, :], vcol(64), A4, ALU.mult, ALU.mult),
                 reads=[R_pb[kg], rA4, R_c], writes=[rA4])
            pread(kg)
            P.op("dve", lambda e, hh=hh: e.tensor_scalar(A0, A0, lbt[:, 16 + hh:17 + hh], lbt[:, 12 + hh:13 + hh],
                                                        ALU.mult, ALU.add),
                 reads=[R_ar[0], R_lb], writes=[R_ar[0]])
            P.op("act", lambda e: e.activation(A1, A0, AF.Ln), reads=[R_ar[0]], writes=[R_ar[1]])
            P.op("dve", lambda e: e.tensor_scalar(A0, A0, -1.0, 1.0, ALU.mult, ALU.add),
                 reads=[R_ar[0], R_ar[1]], writes=[R_ar[0]])
            P.op("dve", lambda e: e.tensor_tensor_scan(A2, scanm, A1, 0.0, ALU.mult, ALU.add),
                 reads=[R_ar[1], R_c], writes=[R_ar[2]])
            A1v = A1.rearrange("p (c j) -> p c j", j=64)
            A2v = A2.rearrange("p (c j) -> p c j", j=64)
            P.op("dve", lambda e: e.tensor_tensor(A1v, A2v, A2v[:, :, 63:64].to_broadcast([128, 8, 64]), ALU.subtract),
                 reads=[R_ar[2], R_ar[1]], writes=[R_ar[1]])
            P.op("act", lambda e, Ddp=Ddp: e.activation(Ddp, A2[:, 63:512:64], AF.Exp), reads=[R_ar[2]], writes=[rDd])
            P.op("act", lambda e: e.activation(A3, A1, AF.Exp, scale=-1.0), reads=[R_ar[1]], writes=[R_ar[3]])
            P.op("act", lambda e: e.activation(A1, A1, AF.Exp), reads=[R_ar[1], R_ar[3]], writes=[R_ar[1]])
            P.op("dve", lambda e, kq=kq, qt=qt: e.tensor_tensor(qt, psb[kq][:, :], A1, ALU.mult),
                 reads=[R_pb[kq], R_ar[1]], writes=[rQK])
            pread(kq)
            P.op("dve", lambda e, kh=kh: e.tensor_tensor(kh, A0, A3, ALU.mult),
                 reads=[R_ar[0], R_ar[3], rQK], writes=[rQK])

        def late(hh):
            par = hh % 2
            A4, rA4 = A4s[par]
            QK, rQK = QKs[par]
            qt, kh = QK[:, 0:512], QK[:, 512:1024]
            Ddp = Dd2[:, par, :]
            rDd = R_Dd2[par]
            itokh = itok2[:, par, :].rearrange("p (b v) -> p b v", b=4)
            rIt = R_itok2[par]
            kt = pb()

            def trk(e, kt=kt, kh=kh):
                ins = None
                for b in range(4):
                    ins = e.transpose(ps_bf(kt)[:, b * 128:(b + 1) * 128], kh[:, b * 128:(b + 1) * 128], ident)
                return ins
            P.op("pe", trk, reads=[rQK, R_c], writes=[R_pb[kt]], cost=0.5)
            P.op("act", lambda e, kt=kt: e.activation(r5[:, 512:1024], ps_bf(kt)[:, 0:512], AF.Copy),
                 reads=[R_pb[kt]], writes=[R_ar[5]])
            pread(kt)
            ks = pb()

            def sc(e, ks=ks, kh=kh, qt=qt):
                ins = None
                for b in range(4):
                    ins = e.matmul(psb[ks][:, b * 128:(b + 1) * 128], kh[:, b * 128:(b + 1) * 128],
                                   qt[:, b * 128:(b + 1) * 128], start=True, stop=True)
                return ins
            P.op("pe", sc, reads=[rQK], writes=[R_pb[ks]], cost=0.5)
            P.op("dve", lambda e, ks=ks: e.tensor_tensor(
                AT, psb[ks][:, :].rearrange("p (b v) -> p b v", b=4),
                hmask.unsqueeze(1).to_broadcast([128, 4, 128]), ALU.mult),
                reads=[R_pb[ks], R_c], writes=[R_ar[6]])
            pread(ks)
            ku = [pb(), pb()]

            def um(e, ku=ku, itokh=itokh):
                ins = None
                for c in range(8):
                    b, pr_ = c // 2, c % 2
                    ins = e.matmul(psb[ku[pr_]][:, b * 128:(b + 1) * 128],
                                   khT[64 * pr_:64 * pr_ + 64, b, :], itokh[64 * pr_:64 * pr_ + 64, b, :],
                                   start=True, stop=True)
                return ins
            P.op("pe", um, reads=[R_ar[5], rIt], writes=[R_pb[ku[0]], R_pb[ku[1]]], cost=0.6)
            for c in range(8):
                P.op("dve", lambda e, c=c, hh=hh, Ddp=Ddp: e.tensor_scalar(Sdb[:, c, :], S32[:, hh, :], Ddp[:, c:c + 1], None, ALU.mult),
                     reads=[R_S[hh], rDd], writes=[R_ar[7]], cost=0.3)
                P.op("dve", lambda e, c=c, hh=hh, ku=ku, Ddp=Ddp: e.scalar_tensor_tensor(
                    S32[:, hh, :], S32[:, hh, :], Ddp[:, c:c + 1], psb[ku[c % 2]][:, (c // 2) * 128:(c // 2 + 1) * 128],
                    ALU.mult, ALU.add),
                    reads=[R_S[hh], rDd, R_pb[ku[c % 2]]], writes=[R_S[hh]], cost=0.35)
            pread(ku[0])
            pread(ku[1])
            ko = pb()

            def om(e, ko=ko, qt=qt, itokh=itokh):
                ins = None
                first = True
                for b in range(4):
                    for c in (2 * b, 2 * b + 1):
                        ins = e.matmul(psb[ko][:, c * 64:(c + 1) * 64], Sdb[:, c, :], qt[:, c * 64:(c + 1) * 64],
                                       start=first, stop=False)
                        first = False
                    ins = e.matmul(psb[ko][:, b * 128:(b + 1) * 128], itokh[:, b, :], AT[:, b, :],
                                   start=False, stop=(b == 3))
                return ins
            P.op("pe", om, reads=[R_ar[7], rQK, rIt, R_ar[6]], writes=[R_pb[ko]], cost=0.9)
            P.op("act", lambda e, ko=ko: e.activation(osq, psb[ko][:, :], AF.Square),
                 reads=[R_pb[ko], R_ar[6]], writes=[R_ar[6]])
            kn = pb()
            P.op("pe", lambda e, kn=kn: e.matmul(psb[kn][:, :], ones_b, osq, start=True, stop=True),
                 reads=[R_ar[6], R_c], writes=[R_pb[kn]])
            P.op("act", lambda e, kn=kn: e.activation(T9, psb[kn][:, :], AF.Ln, bias=eps_ap, scale=1.0 / 128),
                 reads=[R_pb[kn], R_c], writes=[R_ar[9]])
            pread(kn)
            P.op("act", lambda e: e.activation(T9, T9, AF.Exp, scale=-0.5), reads=[R_ar[9]], writes=[R_ar[9]])
            P.op("dve", lambda e, ko=ko: e.tensor_tensor(T9, psb[ko][:, :], T9, ALU.mult),
                 reads=[R_pb[ko], R_ar[9]], writes=[R_ar[9]])
            pread(ko)
            P.op("dve", lambda e, hh=hh, A4=A4: e.tensor_tensor(abT[:, 4 + hh, :], T9, A4, ALU.mult),
                 reads=[R_ar[9], rA4], writes=[R_ab[4 + hh]])

        early(0)
        for hh in range(4):
            if hh + 1 < 4:
                early(hh + 1)
            late(hh)
        if DBG == 3:
            return
        tokmajor_proj(i, 5, abT, R_ab, 0)
        tokmajor_proj(i, 6, abT, R_ab, 1)

    def l1_mixer(i):
        norm_stage(i)
        sl = i % 5
        P.op("pool", lambda e: e.dma_start(out=cosb[:], in_=cos_d[:, i * T:(i + 1) * T]), writes=[R_rope], chan="rope")
        P.op("pool", lambda e: e.dma_start(out=sinb[:], in_=sin_d[:, i * T:(i + 1) * T]), writes=[R_rope], chan="rope")
        X = arena[:, 0, :].bitcast(BF16)
        zb, sq = X[:, 0:512], X[:, 512:1024]
        Y = arena[:, 1, 0:512]
        Z = arena[:, 2, 0:512]
        W = arena[:, 3, 0:512]
        natacc = arena[:, 0:4, 0:512]
        qn = arena[:, 4, :].bitcast(BF16)[:, 0:1024].rearrange("p (j t) -> p j t", j=2)
        qp = arena[:, 5, :].bitcast(BF16)[:, 0:1024].rearrange("p (j t) -> p j t", j=2)
        PT = [arena[:, 6, :].bitcast(BF16)[:, 0:512], arena[:, 6, :].bitcast(BF16)[:, 512:1024],
              arena[:, 7, :].bitcast(BF16)[:, 0:512], arena[:, 7, :].bitcast(BF16)[:, 512:1024]]
        PT_res = [R_ar[6], R_ar[6], R_ar[7], R_ar[7]]
        XS = [(arena[:, 0, :].bitcast(BF16), arena[:, 1, 0:512], arena[:, 2, 0:512], arena[:, 3, 0:512], R_ar[0], R_ar[1], R_ar[2], R_ar[3]),
              (arena[:, 8, :].bitcast(BF16), arena[:, 9, 0:512], arena[:, 10, 0:512], arena[:, 11, 0:512], R_ar[8], R_ar[9], R_ar[10], R_ar[11])]
        for g in range(4):
            (s,) = wuse(i, 27 + 2 * g)
            st1 = {}

            def stage1(wi, g=g, s=s):
                Xb, Y_, Z_, W_, rX, rY, rZ, rW = XS[wi % 2]
                zb_, sq_ = Xb[:, 0:512], Xb[:, 512:1024]
                col0 = wi * 128
                kz = pb()

                def mm(e, kz=kz, col0=col0, s=s):
                    ins = None
                    for kc in range(8):
                        ins = e.matmul(psb[kz][:, :], w3(s)[:, kc, col0:col0 + 128], xnT[:, kc, :],
                                       start=(kc == 0), stop=(kc == 7))
                    return ins
                P.op("pe", mm, reads=[R_xnT, R_ws[s]], writes=[R_pb[kz]])
                P.op("act", lambda e, kz=kz, zb_=zb_: e.activation(zb_, psb[kz][:, :], AF.Copy), reads=[R_pb[kz]], writes=[rX])
                P.op("act", lambda e, kz=kz, sq_=sq_: e.activation(sq_, psb[kz][:, :], AF.Square), reads=[R_pb[kz]], writes=[rX])
                st1[wi] = kz

            def stage2(wi, g=g):
                Xb, Y_, Z_, W_, rX, rY, rZ, rW = XS[wi % 2]
                zb_, sq_ = Xb[:, 0:512], Xb[:, 512:1024]
                isk = wi >= 2
                j = wi % 2
                kz = st1[wi]
                kr, kss = pb(), pb()
                P.op("pe", lambda e, kr=kr, zb_=zb_: e.matmul(psb[kr][:, :], prot, zb_, start=True, stop=True),
                     reads=[rX, R_c], writes=[R_pb[kr]], cost=0.3)
                P.op("pe", lambda e, kss=kss, sq_=sq_: e.matmul(psb[kss][:, :], bd64, sq_, start=True, stop=True),
                     reads=[rX, R_c], writes=[R_pb[kss]], cost=0.3)
                gc = 67 if isk else 65
                P.op("dve", lambda e, kz=kz, gc=gc, Y_=Y_: e.scalar_tensor_tensor(Y_, psb[kz][:, :], vcol(gc), cosb[:], ALU.mult, ALU.mult),
                     reads=[R_pb[kz], R_rope, R_c], writes=[rY])
                pread(kz)
                P.op("dve", lambda e, kr=kr, gc=gc, Z_=Z_: e.scalar_tensor_tensor(Z_, psb[kr][:, :], vcol(gc + 1), sinb[:], ALU.mult, ALU.mult),
                     reads=[R_pb[kr], R_rope, R_c], writes=[rZ])
                pread(kr)
                P.op("act", lambda e, kss=kss, W_=W_: e.activation(W_, psb[kss][:, :], AF.Ln, bias=eps_ap, scale=1.0 / 64),
                     reads=[R_pb[kss], R_c], writes=[rW])
                pread(kss)
                P.op("act", lambda e, W_=W_: e.activation(W_, W_, AF.Exp, scale=-0.5), reads=[rW], writes=[rW])
                P.op("dve", lambda e, Y_=Y_, Z_=Z_: e.tensor_tensor(Y_, Y_, Z_, ALU.add), reads=[rY, rZ], writes=[rY])
                if not isk:
                    dn, dn_r = qn[:, j, :], R_ar[4]
                    dp, dp_r = qp[:, j, :], R_ar[5]
                else:
                    dn, dn_r = Knat[:, 2 * g + j, 128:128 + T], R_KN[g]
                    dp, dp_r = Kperm[:, 2 * g + j, sl * T:(sl + 1) * T], R_KP[sl][g]
                P.op("dve", lambda e, dn=dn, Y_=Y_, W_=W_: e.tensor_tensor(dn, Y_, W_, ALU.mult),
                     reads=[rY, rW], writes=[dn_r])
                P.op("pool", lambda e, dn=dn, dp=dp: e.tensor_copy(
                    dp.rearrange("p (r j) -> p r j", r=4), dn.rearrange("p (j r) -> p r j", r=4)),
                    reads=[dn_r], writes=[dp_r], cost=1.6)

            stage1(0)
            for wi in range(4):
                if wi + 1 < 4:
                    stage1(wi + 1)
                stage2(wi)
            if DBG in (20, 30, 31, 32, 33, 34, 35):
                for _k in range(8):
                    pread(_k)
                continue
            (sv,) = wuse(i, 28 + 2 * g)
            wv = wslot[sv][:, 0:2048].rearrange("p (k n) -> p k n", k=8)
            for half in range(2):
                for perm in (False, True):
                    kv = pb()

                    def mmv(e, kv=kv, half=half, perm=perm, wv=wv):
                        ins = None
                        for bb in range(2):
                            b = half * 2 + bb
                            for kc in range(8):
                                lhs = xnT[:, kc, b:T:4] if perm else xnT[:, kc, b * 128:(b + 1) * 128]
                                ins = e.matmul(psb[kv][:, bb * 256:(bb + 1) * 256], lhs, wv[:, kc, :],
                                               start=(kc == 0), stop=(kc == 7))
                        return ins
                    P.op("pe", mmv, reads=[R_xnT, R_ws[sv]], writes=[R_pb[kv]])
                    if perm:
                        dst = Vperm[:, sl * 4 + half * 2:sl * 4 + half * 2 + 2, 256 * g:256 * g + 256]
                        dr = R_VP[sl][g]
                    else:
                        dst = Vnat[:, 1 + half * 2:3 + half * 2, 256 * g:256 * g + 256]
                        dr = R_VN[g]
                    P.op("act", lambda e, kv=kv, dst=dst: e.activation(
                        dst, psb[kv][:, :].rearrange("p (b n) -> p b n", b=2), AF.Copy),
                        reads=[R_pb[kv]], writes=[dr])
                    pread(kv)
            if DBG == 21:
                for _k in range(8):
                    pread(_k)
                continue
            pairs = []
            acci = 0
            for qb in range(4):
                lst = []
                if not (i == 0 and qb == 0):
                    lst.append((lambda j, qb=qb, g=g: Knat[:, 2 * g + j, qb * 128:(qb + 1) * 128], Vnat[:, qb, :], m1p,
                                [R_KN[g], R_VN[g]]))
                lst.append((lambda j, qb=qb, g=g: Knat[:, 2 * g + j, (qb + 1) * 128:(qb + 2) * 128], Vnat[:, qb + 1, :], m1c,
                            [R_KN[g], R_VN[g]]))
                for n, (kf_, v_, m_, rr_) in enumerate(lst):
                    pairs.append(dict(q=(qn, qb, R_ar[4]), kf=kf_, v=v_, m=m_, rr=rr_, acc=6 + acci % 2,
                                      first=(n == 0), last=(n == len(lst) - 1), nat=True, idx=qb))
                acci += 1
            for c in range(4):
                lst = []
                for mback, msk in ((4, m16f), (3, m16m), (2, m16m), (1, mpp), (0, mpc)):
                    if i - mback < 0:
                        continue
                    s2 = (i - mback) % 5
                    lst.append((lambda j, s2=s2, c=c, g=g: Kperm[:, 2 * g + j, s2 * T + c * 128:s2 * T + (c + 1) * 128],
                                Vperm[:, s2 * 4 + c, :], msk, [R_KP[s2][g], R_VP[s2][g]]))
                for n, (kf_, v_, m_, rr_) in enumerate(lst):
                    pairs.append(dict(q=(qp, c, R_ar[5]), kf=kf_, v=v_, m=m_, rr=rr_, acc=6 + acci % 2,
                                      first=(n == 0), last=(n == len(lst) - 1), nat=False, idx=c))
                acci += 1

            def emit_scores(pr, n):
                kx, ky = pb(), pb()
                qsrc, qidx, qres = pr["q"]

                def scm(e, kx=kx, ky=ky, pr=pr, qsrc=qsrc, qidx=qidx):
                    ins = None
                    for j in range(2):
                        for hp in range(2):
                            bank = kx if hp == 0 else ky
                            ins = e.matmul(psb[bank][:, j * 128:(j + 1) * 128],
                                           pr["kf"](j)[64 * hp:64 * hp + 64, :],
                                           qsrc[64 * hp:64 * hp + 64, j, qidx * 128:(qidx + 1) * 128],
                                           start=True, stop=True)
                    return ins
                P.op("pe", scm, reads=[qres] + pr["rr"], writes=[R_pb[kx], R_pb[ky]], cost=0.35)
                pr["ksc"] = (kx, ky)
                pr["pt"] = n % 4

            def emit_softmax(pr):
                (kx, ky), ptb_, pres = pr["ksc"], PT[pr["pt"]], PT_res[pr["pt"]]
                for hp, kb in ((0, kx), (1, ky)):
                    P.op("act", lambda e, kb=kb, hp=hp, ptb_=ptb_: e.activation(
                        ptb_[:, hp * 256:(hp + 1) * 256], psb[kb][:, 0:256], AF.Exp, scale=0.125),
                        reads=[R_pb[kb]], writes=[pres], cost=0.45)
                    pread(kb)
                P.op("dve", lambda e, ptb_=ptb_, pr=pr: e.tensor_tensor(
                    ptb_.rearrange("p (h q) -> p h q", h=4), ptb_.rearrange("p (h q) -> p h q", h=4),
                    pr["m"].unsqueeze(1).to_broadcast([128, 4, 128]), ALU.mult),
                    reads=[pres, R_c], writes=[pres], cost=0.45)

            def emit_pv(pr):
                ptb_, pres, ka = PT[pr["pt"]], PT_res[pr["pt"]], pr["acc"]

                def pvm(e, ptb_=ptb_, pr=pr, ka=ka, g=g):
                    ins = None
                    for hd in range(4):
                        j, hp = hd // 2, hd % 2
                        rows = slice(64 * hp, 64 * hp + 64)
                        ins = e.matmul(psb[ka][rows, j * 128:(j + 1) * 128],
                                       pr["v"][:, 256 * g + 64 * hd:256 * g + 64 * hd + 64],
                                       ptb_[:, (hp * 2 + j) * 128:(hp * 2 + j + 1) * 128],
                                       start=(pr["first"] and j == 0), stop=False)
                        ins = e.matmul(psb[ka][rows, (2 + j) * 128:(3 + j) * 128],
                                       ones_b[:, 0:64], ptb_[:, (hp * 2 + j) * 128:(hp * 2 + j + 1) * 128],
                                       start=False, stop=(pr["last"] and j == 1))
                    return ins
                P.op("pe", pvm, reads=[pres, R_c] + pr["rr"], writes=[R_pb[ka]], cost=0.7)
                if pr["last"]:
                    if pr["nat"]:
                        qb = pr["idx"]
                        P.op("act", lambda e, ka=ka, qb=qb: e.activation(
                            natacc[:, :, qb * 128:(qb + 1) * 128], psb[ka][:, :].rearrange("p (k q) -> p k q", k=4), AF.Copy),
                            reads=[R_pb[ka]], writes=R_ar[0:4])
                    else:
                        c = pr["idx"]
                        P.op("dve", lambda e, ka=ka, c=c: e.tensor_tensor(
                            tot[:], psb[ka][:, :].rearrange("p (k q) -> p k q", k=4), natacc[:, :, c:T:4], ALU.add),
                            reads=[R_pb[ka]] + R_ar[0:4], writes=[R_tot])
                        P.op("dve", lambda e: e.reciprocal(tot[:, 2:4, :], tot[:, 2:4, :]), reads=[R_tot], writes=[R_tot])
                        P.op("dve", lambda e, c=c, g=g: e.tensor_tensor(
                            abT[:, 2 * g:2 * g + 2, c:T:4], tot[:, 0:2, :], tot[:, 2:4, :], ALU.mult),
                            reads=[R_tot], writes=[R_ab[2 * g], R_ab[2 * g + 1]])

            for n, pr in enumerate(pairs):
                emit_scores(pr, n)
                emit_softmax(pr)
                if n >= 2:
                    emit_pv(pairs[n - 2])
            emit_pv(pairs[-2])
            emit_pv(pairs[-1])
            P.op("pool", lambda e, g=g: e.tensor_copy(Knat[:, 2 * g:2 * g + 2, 0:128], Knat[:, 2 * g:2 * g + 2, T:T + 128]),
                 reads=[R_KN[g]], writes=[R_KN[g]])
            P.op("pool", lambda e, g=g: e.tensor_copy(Vnat[:, 0, 256 * g:256 * g + 256], Vnat[:, 4, 256 * g:256 * g + 256]),
                 reads=[R_VN[g]], writes=[R_VN[g]])
        if DBG == 40:
            P.op("dve", lambda e: e.tensor_copy(h[:].rearrange("p b d -> p (b d)"), abT[:].rearrange("p c t -> p (c t)")),
                 reads=R_ab, writes=R_h)
            return
        if DBG == 41:
            P.op("dve", lambda e: e.tensor_copy(h[:].rearrange("p b d -> p (b d)"), Knat[:, :, 128:640].rearrange("p c t -> p (c t)")),
                 reads=R_KN, writes=R_h)
            return
        if DBG == 42:
            P.op("dve", lambda e: e.tensor_copy(h[:].rearrange("p b d -> p (b d)"), Vnat[:, 1:5, :].rearrange("p c t -> p (c t)")),
                 reads=R_VN, writes=R_h)
            return
        tokmajor_proj(i, 35, abT, R_ab, 0)
        tokmajor_proj(i, 36, abT, R_ab, 1)


    for i in range(nt):
        for b in range(4):
            P.op("pool", lambda e, b=b, i=i: e.dma_start(out=h[:, b, :], in_=x_d[i * T + b * 128:i * T + (b + 1) * 128, :]),
                 writes=[R_h[b]], chan=f"xin{b}")
        if upto >= 1:
            P.tag = f"{i}:l0mix"
            l0_mixer(i)
        if upto >= 2:
            P.tag = f"{i}:mlp0"
            mlp_stage(i, 7)
        if upto >= 3:
            P.tag = f"{i}:ple0"
            ple_stage(i, 23, 0)
        if upto >= 4:
            P.tag = f"{i}:l1mix"
            l1_mixer(i)
        if upto >= 5:
            P.tag = f"{i}:mlp1"
            mlp_stage(i, 37)
        if upto >= 6:
            P.tag = f"{i}:ple1"
            ple_stage(i, 53, 1)
        P.tag = f"{i}:io"
        for b in range(4):
            P.op("pool", lambda e, b=b, i=i: e.dma_start(out=out_d[i * T + b * 128:i * T + (b + 1) * 128, :], in_=h[:, b, :]),
                 reads=[R_h[b]], chan=f"out{b}")

    sems = {k: es.enter_context(nc.semaphore("s_" + k)) for k in ("pe", "act", "dve", "pool")}
    if SCHED:
        P.schedule()
        print('sched sim_time us', P.sim_time, 'nops', len(P.ops), flush=True)
    else:
        P.by_eng = {e: [x for x in P.ops if x.eng == e] for e in ENGS}
    chan_names = sorted({x.chan for x in P.ops if x.chan is not None})
    chs = {k: es.enter_context(nc.semaphore("c_" + k)) for k in chan_names}
    with nc.Block() as block:
        P.emit(block, sems, chs, final_waits=[f"out{b}" for b in range(4)])
    es.close()
    return nc


def run(inputs, nt=SEQ // T, upto=6, n_cores=8, trace=False):
    S = nt * T
    ws = _pack_weights(inputs)
    vecs = _pack_vec(inputs)
    cbf, cf, cosd, sind = _consts(S)
    pool_w = np.ascontiguousarray(inputs["pool_w"][0])
    nc = build(nt, upto)
    x, p = inputs["x"], inputs["p"]
    in_maps = []
    for c in range(n_cores):
        in_maps.append({
            "x": np.ascontiguousarray(x[c, :S]), "p": np.ascontiguousarray(p[:, c, :S]),
            "wsrc": ws, "vec": vecs, "cbf": cbf, "cf": cf, "cosd": cosd, "sind": sind, "pool_w": pool_w,
        })
    res = run_bass_kernel_spmd(nc, in_maps, core_ids=list(range(n_cores)), trace=trace)
    out = np.stack([res.results[c]["out"] for c in range(n_cores)], axis=0)
    return out, res


def kernel(**inputs):
    inputs = {k: np.asarray(v) for k, v in inputs.items()}
    out, _ = run(inputs)
    return out.astype(np.float32)
```

```python
import numpy as np
from contextlib import ExitStack
import concourse.bass as bass
import concourse.mybir as mybir
from concourse.bass_utils import run_bass_kernel_spmd

F32 = mybir.dt.float32
BF16 = mybir.dt.bfloat16
AF = mybir.ActivationFunctionType
ALU = mybir.AluOpType

D = 1024
SEQ = 8192
T = 512
NJ = 57
SLOT = 4096
EPS = 1e-6
ENGS = ("pe", "act", "dve", "pool", "sp")
DEF_COST = {"pe": 2.2, "act": 0.65, "dve": 0.7, "pool": 1.2, "sp": 4.0}
SYNC_LAT = 0.15
SCHED = True
PRIO = "cp"
DBG = 0


class Res:
    __slots__ = ("name", "w", "rs", "const", "excl")

    def __init__(self, name, const=False, excl=False):
        self.name = name
        self.w = None
        self.rs = []
        self.const = const
        self.excl = excl


class Op:
    __slots__ = ("eng", "fn", "deps", "marked", "count", "chan", "chan_count", "cost", "idx", "tag", "st", "ft")


class Prog:
    def __init__(self):
        self.ops = []
        self.by_eng = {e: [] for e in ENGS}
        self.chan_counts = {}

    def op(self, eng, fn, reads=(), writes=(), chan=None, cost=None):
        x = Op()
        x.eng = eng
        if cost is None:
            cost = 4.0 if chan is not None else DEF_COST[eng]
        x.cost = cost
        x.idx = len(self.ops)
        x.tag = getattr(self, "tag", "")
        x.fn = fn
        x.marked = False
        x.count = 0
        x.chan = chan
        x.chan_count = 0
        deps = []
        seen = set()

        def add(y):
            if y is not None and id(y) not in seen:
                seen.add(id(y))
                deps.append(y)

        writes = list(writes) + [r for r in reads if r.excl]
        reads = [r for r in reads if not r.excl]
        for r in reads:
            add(r.w)
        for w in writes:
            add(w.w)
            for y in w.rs:
                add(y)
        for r in reads:
            if not r.const:
                r.rs.append(x)
        for w in writes:
            w.w = x
            w.rs = []
        x.deps = deps
        self.ops.append(x)
        return x

    def schedule(self):
        import heapq
        ops = self.ops
        n = len(ops)
        succ = [[] for _ in range(n)]
        indeg = [0] * n
        for x in ops:
            indeg[x.idx] = len(x.deps)
            for y in x.deps:
                succ[y.idx].append(x.idx)
        bl = [0.0] * n
        for i in range(n - 1, -1, -1):
            m = 0.0
            for j in succ[i]:
                if bl[j] > m:
                    m = bl[j]
            bl[i] = ops[i].cost + m + (SYNC_LAT if succ[i] else 0.0)
        if PRIO == "cp":
            prio = [-b for b in bl]
        else:
            prio = list(range(n))
        ready_t = [0.0] * n
        fin = [0.0] * n
        free = {e: 0.0 for e in ENGS}
        pend = {e: [] for e in ENGS}
        avail = {e: [] for e in ENGS}
        for x in ops:
            if indeg[x.idx] == 0:
                heapq.heappush(pend[x.eng], (0.0, prio[x.idx], x.idx))
        order = {e: [] for e in ENGS}
        done = 0
        while done < n:
            best = None
            for e in ENGS:
                pe_, av = pend[e], avail[e]
                while pe_ and pe_[0][0] <= free[e]:
                    t_ = heapq.heappop(pe_)
                    heapq.heappush(av, (t_[1], t_[2]))
                if av:
                    cand = (free[e], av[0][0], e, True)
                elif pe_:
                    cand = (pe_[0][0], pe_[0][1], e, False)
                else:
                    continue
                if best is None or cand[:2] < best[:2]:
                    best = cand
            st, _p, e, from_av = best
            if from_av:
                i = heapq.heappop(avail[e])[1]
            else:
                i = heapq.heappop(pend[e])[2]
            x = ops[i]
            if x.chan is not None:
                free[e] = st + 0.15
                fin[i] = st + x.cost
            else:
                free[e] = st + x.cost
                fin[i] = free[e]
            order[e].append(x)
            x.st, x.ft = st, fin[i]
            done += 1
            for j in succ[i]:
                if fin[i] > ready_t[j]:
                    ready_t[j] = fin[i]
                indeg[j] -= 1
                if indeg[j] == 0:
                    heapq.heappush(pend[ops[j].eng], (ready_t[j] + SYNC_LAT, prio[j], j))
        self.by_eng = order
        self.sim_time = max(fin) if n else 0.0

    def assign_chans(self):
        self.chan_counts = {}
        allops = []
        for e in ENGS:
            allops.extend(self.by_eng[e])
        for e in ENGS:
            for x in self.by_eng[e]:
                if x.chan is not None:
                    self.chan_counts[x.chan] = self.chan_counts.get(x.chan, 0) + 16
                    x.chan_count = self.chan_counts[x.chan]

    def finalize(self):
        for x in self.ops:
            for y in x.deps:
                if y.chan is not None:
                    continue
                if y.eng == "pe" and x.eng == "pe":
                    continue
                y.marked = True
        for e in ENGS:
            c = 0
            for x in self.by_eng[e]:
                if x.marked:
                    c += 1
                    x.count = c

    def emit(self, block, sems, chan_sems, final_waits=()):
        self.finalize()
        self.assign_chans()
        engs = {"pe": block.tensor, "act": block.scalar, "dve": block.vector,
                "pool": block.gpsimd, "sp": block.sync}
        prog = self

        def make(ename):
            def body(e):
                waited = {}
                for x in prog.by_eng[ename]:
                    for y in x.deps:
                        if y.chan is not None:
                            key, val, sem = "c:" + y.chan, y.chan_count, chan_sems[y.chan]
                        else:
                            if y.eng == "pe" and ename == "pe":
                                continue
                            key, val, sem = y.eng, y.count, sems[y.eng]
                        if waited.get(key, 0) < val:
                            e.wait_ge(sem, val)
                            waited[key] = val
                    ins = x.fn(e)
                    if x.chan is not None:
                        ins.then_inc(chan_sems[x.chan], 16)
                    elif x.marked:
                        ins.then_inc(sems[ename], 1)
                if ename == "sp":
                    for ch in final_waits:
                        e.wait_ge(chan_sems[ch], prog.chan_counts[ch])
            return body

        for ename in ENGS:
            engs[ename](make(ename))


def _slotify(w, rows, c0, nc_):
    out = np.empty((128, len(rows), nc_), np.float32)
    for k, r0 in enumerate(rows):
        out[:, k, :] = w[r0:r0 + 128, c0:c0 + nc_]
    return out.reshape(128, len(rows) * nc_)


def _pack_weights(inp):
    ws = np.zeros((NJ, 128, SLOT), np.float32)
    r8 = [k * 128 for k in range(8)]
    w_in = inp["w_in_ab"][0]
    ws[0] = _slotify(w_in, r8, 0, 512)
    for hh in range(4):
        t = np.empty((128, 8, 512), np.float32)
        for k in range(8):
            rows = slice(k * 128, k * 128 + 128)
            t[:, k, 0:128] = w_in[rows, 512 + hh * 128:512 + hh * 128 + 128]
            t[:, k, 128:256] = w_in[rows, 1024 + hh * 128:1024 + hh * 128 + 128]
            t[:, k, 256:384] = w_in[rows, 2048 + hh * 128:2048 + hh * 128 + 128]
            t[:, k, 384:512] = w_in[rows, 1536 + hh * 128:1536 + hh * 128 + 128]
        ws[1 + hh] = t.reshape(128, SLOT)
    w_out = inp["w_out_ab"][0]
    ws[5] = _slotify(w_out, r8, 0, 512)
    ws[6] = _slotify(w_out, r8, 512, 512)

    def mlp_ple(base, l):
        wu, wd = inp["w_up"][l], inp["w_down"][l]
        j = base
        for qd in range(4):
            for hf in range(2):
                ws[j] = _slotify(wu, r8, qd * 1024 + hf * 512, 512)
                j += 1
            for nh in range(2):
                ws[j] = _slotify(wd, [qd * 1024 + k * 128 for k in range(8)], nh * 512, 512)
                j += 1
        wp, wg = inp["w_ple"][l], inp["w_ple_gate"][l]
        for nh in range(2):
            ws[j][:, 0:1024] = _slotify(wp, [0, 128], nh * 512, 512)
            j += 1
            ws[j] = _slotify(wg, r8, nh * 512, 512)
            j += 1
        return j

    j = mlp_ple(7, 0)
    assert j == 27
    wqkv = inp["w_qkv"][0]
    for g in range(4):
        t = np.empty((128, 8, 512), np.float32)
        for k in range(8):
            rows = slice(k * 128, k * 128 + 128)
            t[:, k, 0:256] = wqkv[rows, 256 * g:256 * g + 256]
            t[:, k, 256:512] = wqkv[rows, 1024 + 256 * g:1024 + 256 * g + 256]
        ws[27 + 2 * g] = t.reshape(128, SLOT)
        ws[28 + 2 * g][:, 0:2048] = _slotify(wqkv, r8, 2048 + 256 * g, 256)
    wo = inp["w_o"][0]
    ws[35] = _slotify(wo, r8, 0, 512)
    ws[36] = _slotify(wo, r8, 512, 512)
    j = mlp_ple(37, 1)
    assert j == NJ
    return ws


def _col8(v):
    return np.ascontiguousarray(v.reshape(8, 128).T)


def _pack_vec(inp):
    cols = []
    for nm, l in (("mix_norm", 0), ("mlp_norm", 0), ("ple_norm", 0),
                  ("mix_norm", 1), ("mlp_norm", 1), ("ple_norm", 1)):
        cols.append(_col8(inp[nm][l]))
    cols.append(np.ascontiguousarray(inp["pool_scale"][0].reshape(4, 128).T))
    lb = inp["hgrn_lb"]
    cols.append(np.ascontiguousarray(lb.reshape(3, 4, 128).transpose(2, 0, 1).reshape(128, 12)))
    cols.append(inp["hgrn_o_norm"][0].reshape(128, 1))
    idx = np.arange(128) % 64
    sw = (idx + 32) % 64
    qn, kn = inp["q_norm"][0], inp["k_norm"][0]
    cols.append(np.stack([qn[idx], qn[sw], kn[idx], kn[sw]], axis=1))
    return np.ascontiguousarray(np.concatenate(cols, axis=1).astype(np.float32))


NV = 69
NCB = 11 * 128
NCF = 512 + 64 + 1


def _consts(seq):
    a = np.arange(128)
    kp, qi = a[:, None], a[None, :]
    ident = (kp == qi)
    ones = np.ones((128, 128))
    bd64 = (kp // 64 == qi // 64)
    prot = np.zeros((128, 128))
    for m in range(128):
        if m % 64 < 32:
            prot[m + 32, m] = -1.0
        else:
            prot[m - 32, m] = 1.0
    hmask = (kp <= qi) & (kp // 64 == qi // 64)
    m1p = kp >= qi
    m1c = kp <= qi
    same = ((kp - qi) % 4 == 0)
    m16f = same & (kp >= qi)
    m16m = same
    m16c = same & (kp <= qi)
    mpp = m1p.astype(np.float32) + m16m
    mpc = m1c.astype(np.float32) + m16c
    cbf = np.concatenate([np.asarray(x, np.float32) for x in
                          (ident, ones, bd64, prot, hmask, m1p, m1c, m16f, m16m, mpp, mpc)], axis=1)
    scanm = np.ones((128, 512), np.float32)
    scanm[:, ::64] = 0.0
    invc = np.zeros((128, 64), np.float32)
    for g, w in enumerate((2, 4, 8, 16)):
        invc[:, g * 16:(g + 1) * 16] = 1.0 / np.minimum(np.arange(16) + 1, w)
    cf = np.concatenate([scanm, invc, np.full((128, 1), EPS, np.float32)], axis=1)
    half = 32
    inv = (10000.0 ** (-np.arange(half, dtype=np.float32) / half)).astype(np.float32)
    pos = np.arange(seq, dtype=np.float32)
    ang = (pos[None, :] * inv[(np.arange(128) % 64) % 32][:, None]).astype(np.float32)
    return (np.ascontiguousarray(cbf.astype(np.float32)), np.ascontiguousarray(cf),
            np.cos(ang).astype(np.float32), np.sin(ang).astype(np.float32))


def build(nt, upto=6):
    S = nt * T
    nc = bass.Bass("TRN2", target_bir_lowering=False)
    P = Prog()
    es = ExitStack()

    def dram(name, shape, dt, kind):
        return nc.dram_tensor(name, shape, dt, kind=kind).ap()

    global LAST_PROG
    LAST_PROG = P
    x_d = dram("x", [S, D], F32, "ExternalInput")
    p_d = dram("p", [2, S, 256], F32, "ExternalInput")
    wsrc = dram("wsrc", [NJ, 128, SLOT], F32, "ExternalInput")
    vec_d = dram("vec", [128, NV], F32, "ExternalInput")
    cbf_d = dram("cbf", [128, NCB], F32, "ExternalInput")
    cf_d = dram("cf", [128, NCF], F32, "ExternalInput")
    cos_d = dram("cosd", [128, S], F32, "ExternalInput")
    sin_d = dram("sind", [128, S], F32, "ExternalInput")
    out_d = dram("out", [S, D], F32, "ExternalOutput")
    wsc = dram("wsc", [NJ, 128, SLOT], BF16, "Internal")

    def sb(name, shape, dt):
        return es.enter_context(nc.sbuf_tensor(name, shape, dt))

    h = sb("h", [128, 4, D], F32)
    xtok2 = [sb(f"xtok{q}", [128, D], BF16) for q in range(2)]
    xnT = sb("xnT", [128, 8, T], BF16)
    abT = sb("abT", [128, 8, T], BF16)
    arena = sb("arena", [128, 14, 528], F32)
    wslot = [sb(f"wslot{i}", [128, SLOT], BF16) for i in range(3)]
    Kperm = sb("Kperm", [128, 8, 5 * T], BF16)
    Vperm = sb("Vperm", [128, 20, D], BF16)
    Knat = sb("Knat", [128, 8, 128 + T], BF16)
    Vnat = sb("Vnat", [128, 5, D], BF16)
    cosb = sb("cosb", [128, T], F32)
    sinb = sb("sinb", [128, T], F32)
    ptok = arena[:, 6, :].bitcast(BF16)[:, 0:1024].rearrange("p (b f) -> p b f", b=4)
    pT = arena[:, 7, :].bitcast(BF16)[:, 0:1024].rearrange("p (c t) -> p c t", c=2)
    cbf = sb("cbf_sb", [128, NCB], BF16)
    cf = sb("cf_sb", [128, NCF], F32)
    vec = sb("vecs", [128, NV], F32)
    S32 = sb("S32", [128, 4, 128], F32)
    uh = sb("uh", [128, 4, 16], F32)
    stat = sb("stat", [128, 32], F32)
    lbt = sb("lbt", [128, 24], F32)
    Dd2 = sb("Dd2", [128, 2, 8], F32)
    itok2 = sb("itok2", [128, 2, 512], BF16)
    tot = sb("tot", [128, 4, 128], F32)
    psb = [es.enter_context(nc.psum_tensor(f"ps{i}", [128, 512], F32)) for i in range(8)]

    R_h = [Res(f"h{b}") for b in range(4)]
    R_xtok2 = [Res("xtok0"), Res("xtok1")]
    R_xnT = Res("xnT")
    R_ab = [Res(f"ab{c}") for c in range(8)]
    R_ar = [Res(f"ar{r}") for r in range(14)]
    R_ws = [Res(f"ws{i}") for i in range(3)]
    R_wsc = [Res(f"wsc{j}") for j in range(NJ)]
    R_prep = [Res(f"prep{j}") for j in range(4)]
    R_KP = [[Res(f"kp{s}_{g}") for g in range(4)] for s in range(5)]
    R_VP = [[Res(f"vp{s}_{g}") for g in range(4)] for s in range(5)]
    R_KN = [Res(f"kn{g}") for g in range(4)]
    R_VN = [Res(f"vn{g}") for g in range(4)]
    R_rope = Res("rope")
    R_ptok = R_ar[6]
    R_pT = R_ar[7]
    R_c = Res("consts", const=True)
    R_S = [Res(f"S{hh}") for hh in range(4)]
    R_uh = Res("uh")
    R_stat4 = [Res(f"stat{b}") for b in range(4)]
    R_lb = Res("lb", const=True)
    R_Dd2 = [Res("Dd0"), Res("Dd1")]
    R_itok2 = [Res("itok0"), Res("itok1")]
    R_tot = Res("tot")
    R_pb = [Res(f"pb{i}", excl=True) for i in range(8)]

    def cb(i):
        return cbf[:, i * 128:(i + 1) * 128]

    ident, ones_b, bd64, prot, hmask = cb(0), cb(1), cb(2), cb(3), cb(4)
    m1p, m1c, m16f, m16m, mpp, mpc = cb(5), cb(6), cb(7), cb(8), cb(9), cb(10)
    scanm = cf[:, 0:512]
    eps_ap = cf[:, 576:577]

    def vcol(i):
        return vec[:, i:i + 1]

    pstate = {"next": 0, "unread": [False] * 8}

    def pb():
        i = pstate["next"]
        pstate["next"] = (i + 1) % 6
        assert not pstate["unread"][i], f"psum bank {i} reallocated before read"
        pstate["unread"][i] = True
        return i

    def pread(i):
        pstate["unread"][i] = False

    def ps_bf(i):
        return psb[i][:].bitcast(BF16)

    R_c = Res("consts", const=True)
    P.op("pool", lambda e: e.dma_start(out=cbf[:], in_=cbf_d[:, :]), writes=[R_c], chan="constp")
    P.op("sp", lambda e: e.dma_start(out=cf[:], in_=cf_d[:, :]), reads=[R_c], writes=[R_c], chan="const")
    P.op("sp", lambda e: e.dma_start(out=vec[:], in_=vec_d[:, :]), reads=[R_c], writes=[R_c], chan="const")
    pw_sb = sb("pw_sb", [128, 4, 128], BF16)
    pool_w_d = dram("pool_w", [4, 128, 128], F32, "ExternalInput")
    P.op("pool", lambda e: e.dma_start(out=pw_sb[:], in_=pool_w_d.rearrange("g c d -> c g d")), reads=[R_c], writes=[R_c], chan="constp")
    P.op("dve", lambda e: e.memset(S32[:], 0.0), writes=R_S)
    P.op("dve", lambda e: e.memset(uh[:], 0.0), writes=[R_uh])
    P.op("act", lambda e: e.activation(lbt[:, 0:12], vec[:, 52:64], AF.Exp), reads=[R_c], writes=[R_lb])
    P.op("dve", lambda e: e.tensor_tensor(lbt[:, 20:24], lbt[:, 0:4], lbt[:, 4:8], ALU.add), reads=[R_lb], writes=[R_lb])
    P.op("dve", lambda e: e.tensor_tensor(lbt[:, 20:24], lbt[:, 20:24], lbt[:, 8:12], ALU.add), reads=[R_lb], writes=[R_lb])
    P.op("dve", lambda e: e.reciprocal(lbt[:, 20:24], lbt[:, 20:24]), reads=[R_lb], writes=[R_lb])
    P.op("dve", lambda e: e.tensor_tensor(lbt[:, 12:16], lbt[:, 0:4], lbt[:, 20:24], ALU.mult), reads=[R_lb], writes=[R_lb])
    P.op("dve", lambda e: e.tensor_scalar(lbt[:, 16:20], lbt[:, 12:16], -1.0, 1.0, ALU.mult, ALU.add), reads=[R_lb], writes=[R_lb])

    gain_of = {}
    for j in range(0, 5):
        gain_of[j] = 0
    for base, l in ((7, 0), (37, 1)):
        for qd in range(4):
            gain_of[base + 4 * qd] = 1 + 3 * l
            gain_of[base + 4 * qd + 1] = 1 + 3 * l
        gain_of[base + 17] = 2 + 3 * l
        gain_of[base + 19] = 2 + 3 * l
    for j in range(27, 35):
        gain_of[j] = 3
    nel = {j: SLOT for j in range(NJ)}
    for j in (23, 25, 53, 55):
        nel[j] = 1024
    for j in (28, 30, 32, 34):
        nel[j] = 2048
    stg_state = {"n": 0}
    R_thr = [Res(f"thr{j}") for j in range(NJ)]
    PREP_AHEAD = 6

    def emit_prep(j):
        n = nel[j]
        thr = [R_thr[j - PREP_AHEAD]] if j >= PREP_AHEAD else []
        if j in gain_of:
            stg_n = stg_state["n"]
            s = 1 + stg_n % 3
            use_v = (stg_n // 3) % 2 == 1
            stg_state["n"] += 1
            ncol = n // 8
            if use_v:
                stg = Vperm[:, 4 * s:4 * s + 4, :].rearrange("p a (b n) -> p (a b) n", b=2)[:, :, 0:ncol]
                stg_res = R_VP[s]
            else:
                stg = Kperm[:, :, s * T:s * T + ncol]
                stg_res = R_KP[s]
            tagc = ("v" if use_v else "k") + str(s)
            P.op("pool", lambda e, j=j, stg=stg, n=n: e.dma_start(out=stg, in_=wsrc[j, :, 0:n].rearrange("p (k n) -> p k n", k=8)),
                 reads=thr, writes=stg_res, chan=f"pw{tagc}")
            gi = gain_of[j]
            P.op("dve", lambda e, stg=stg, gi=gi, ncol=ncol: e.tensor_tensor(
                stg, stg, vec[:, gi * 8:gi * 8 + 8].unsqueeze(2).to_broadcast([128, 8, ncol]), ALU.mult),
                reads=[R_c] + stg_res, writes=stg_res, cost=2.4)
            P.op("sp", lambda e, j=j, stg=stg, n=n: e.dma_start(out=wsc[j, :, 0:n].rearrange("p (k n) -> p k n", k=8), in_=stg),
                 reads=stg_res, writes=[R_wsc[j]], chan=f"ps{tagc}")
        else:
            P.op("pool", lambda e, j=j, n=n: e.dma_start(out=wsc[j, :, 0:n], in_=wsrc[j, :, 0:n]),
                 reads=thr, writes=[R_wsc[j], R_prep[j % 4]], chan=f"prep{j % 4}")

    for j in range(min(PREP_AHEAD, NJ)):
        emit_prep(j)

    wst = {"loaded": -1}

    def wuse(i, k_first, k_last=None):
        if k_last is None:
            k_last = k_first
        n_first = i * NJ + k_first
        n_last = i * NJ + k_last
        assert n_last <= n_first + 2
        lim = min(n_first + 2, nt * NJ - 1)
        while wst["loaded"] < lim:
            n = wst["loaded"] + 1
            jt, s = n % NJ, n % 3
            ne = nel[jt]
            P.op("sp", lambda e, jt=jt, s=s, ne=ne: e.dma_start(out=wslot[s][:, 0:ne], in_=wsc[jt, :, 0:ne]),
                 reads=[R_wsc[jt]], writes=[R_ws[s]] + ([R_thr[n]] if n < NJ else []), chan=f"w{s}")
            wst["loaded"] = n
            if n + PREP_AHEAD < NJ:
                emit_prep(n + PREP_AHEAD)
        return [((i * NJ + k) % 3) for k in range(k_first, k_last + 1)]

    def w3(s, ncol=512):
        return wslot[s][:, 0:8 * ncol].rearrange("p (k n) -> p k n", k=8)

    def norm_stage(i):
        for b in range(4):
            xtok = xtok2[b % 2]
            R_xtok = R_xtok2[b % 2]
            R_stat = R_stat4[b]
            P.op("pool", lambda e, b=b: e.memset(stat[:, b:b + 1], 0.0), writes=[R_stat], cost=0.1)
            P.op("act", lambda e, b=b, xtok=xtok: e.activation(xtok[:], h[:, b, :], AF.Square, accum_out=stat[:, b:b + 1]),
                 reads=[R_h[b], R_stat], writes=[R_xtok, R_stat], cost=1.1)
            P.op("act", lambda e, b=b: e.activation(stat[:, 8 + b:9 + b], stat[:, b:b + 1], AF.Ln, bias=eps_ap, scale=1.0 / D),
                 reads=[R_stat, R_c], writes=[R_stat], cost=0.3)
            P.op("act", lambda e, b=b: e.activation(stat[:, 16 + b:17 + b], stat[:, 8 + b:9 + b], AF.Exp, scale=-0.5),
                 reads=[R_stat], writes=[R_stat], cost=0.3)
            P.op("dve", lambda e, b=b, xtok=xtok: e.tensor_scalar(xtok[:], h[:, b, :], stat[:, 16 + b:17 + b], None, ALU.mult),
                 reads=[R_h[b], R_stat, R_xtok], writes=[R_xtok], cost=1.2)
            k = pb()

            def tr(e, k=k, xtok=xtok):
                ins = None
                for c in range(8):
                    ins = e.transpose(ps_bf(k)[:, c * 128:(c + 1) * 128], xtok[:, c * 128:(c + 1) * 128], ident)
                return ins
            P.op("pe", tr, reads=[R_xtok, R_c], writes=[R_pb[k]], cost=1.0)
            P.op("act", lambda e, k=k, b=b: e.activation(
                xnT[:, :, b * 128:(b + 1) * 128], ps_bf(k).rearrange("p (c t) -> p c t", c=8), AF.Copy),
                reads=[R_pb[k]], writes=[R_xnT], cost=1.05)
            pread(k)

    def tokmajor_proj(i, jk, src, src_res, nh, first_dst=None):
        (s,) = wuse(i, jk)
        for b in range(4):
            k = pb()

            def mm(e, k=k, b=b, s=s):
                ins = None
                for kc in range(8):
                    ins = e.matmul(psb[k][:, :], src[:, kc, b * 128:(b + 1) * 128], w3(s)[:, kc, :],
                                   start=(kc == 0), stop=(kc == 7))
                return ins
            P.op("pe", mm, reads=list(src_res) + [R_ws[s]], writes=[R_pb[k]])
            P.op("dve", lambda e, k=k, b=b, nh=nh: e.tensor_tensor(
                h[:, b, nh * 512:(nh + 1) * 512], h[:, b, nh * 512:(nh + 1) * 512], psb[k][:, :], ALU.add),
                reads=[R_pb[k], R_h[b]], writes=[R_h[b]])
            pread(k)

    def mlp_stage(i, base):
        norm_stage(i)
        hid = [arena[:, 0:4, :].bitcast(BF16), arena[:, 4:8, :].bitcast(BF16)]
        hid_res = [R_ar[0:4], R_ar[4:8]]

        def hv(par, c):
            return hid[par][:, c // 2, (c % 2) * 512:(c % 2) * 512 + 512]

        for qd in range(4):
            par = qd % 2
            for hf in range(2):
                (s,) = wuse(i, base + 4 * qd + hf)
                for m in range(4):
                    k = pb()
                    c = hf * 4 + m

                    def mm(e, k=k, m=m, s=s):
                        ins = None
                        for kc in range(8):
                            ins = e.matmul(psb[k][:, :], w3(s)[:, kc, m * 128:(m + 1) * 128], xnT[:, kc, :],
                                           start=(kc == 0), stop=(kc == 7))
                        return ins
                    P.op("pe", mm, reads=[R_xnT, R_ws[s]], writes=[R_pb[k]])
                    rr = hid_res[par][c // 2]
                    P.op("act", lambda e, k=k, par=par, c=c: e.activation(hv(par, c), psb[k][:, :], AF.Relu),
                         reads=[R_pb[k]], writes=[rr])
                    pread(k)
                    P.op("pool", lambda e, par=par, c=c: e.tensor_tensor(hv(par, c), hv(par, c), hv(par, c), ALU.mult),
                         reads=[rr], writes=[rr], cost=1.1)
            for nh in range(2):
                (s,) = wuse(i, base + 4 * qd + 2 + nh)
                for b in range(4):
                    k = pb()

                    def mm(e, k=k, b=b, s=s, par=par):
                        ins = None
                        for kc in range(8):
                            ins = e.matmul(psb[k][:, :], hv(par, kc)[:, b * 128:(b + 1) * 128], w3(s)[:, kc, :],
                                           start=(kc == 0), stop=(kc == 7))
                        return ins
                    P.op("pe", mm, reads=hid_res[par] + [R_ws[s]], writes=[R_pb[k]])
                    P.op("dve", lambda e, k=k, b=b, nh=nh: e.tensor_tensor(
                        h[:, b, nh * 512:(nh + 1) * 512], h[:, b, nh * 512:(nh + 1) * 512], psb[k][:, :], ALU.add),
                        reads=[R_pb[k], R_h[b]], writes=[R_h[b]])
                    pread(k)

    def ple_stage(i, base, l):
        norm_stage(i)
        P.op("pool", lambda e: e.dma_start(
            out=ptok, in_=p_d[l, i * T:(i + 1) * T, :].rearrange("(b p) f -> p b f", p=128)),
            writes=[R_ptok], chan="pld")
        k = pb()

        def tr(e, k=k):
            ins = None
            for pc in range(2):
                for b in range(4):
                    ins = e.transpose(ps_bf(k)[:, pc * 512 + b * 128:pc * 512 + (b + 1) * 128],
                                      ptok[:, b, pc * 128:(pc + 1) * 128], ident)
            return ins
        P.op("pe", tr, reads=[R_ptok, R_c], writes=[R_pb[k]])
        P.op("act", lambda e, k=k: e.activation(arena[:, 7, :].bitcast(BF16)[:, 0:1024], ps_bf(k), AF.Copy),
             reads=[R_pb[k]], writes=[R_pT])
        pread(k)
        for nh in range(2):
            sp_, sg = wuse(i, base + 2 * nh, base + 2 * nh + 1)
            for b in range(4):
                par = (nh * 4 + b) % 2
                gt = arena[:, 2 * par, 0:512]
                pw = arena[:, 2 * par + 1, 0:512]
                rg, rp = R_ar[2 * par], R_ar[2 * par + 1]
                kg = pb()

                def mmg(e, kg=kg, b=b, sg=sg):
                    ins = None
                    for kc in range(8):
                        ins = e.matmul(psb[kg][:, :], xnT[:, kc, b * 128:(b + 1) * 128], w3(sg)[:, kc, :],
                                       start=(kc == 0), stop=(kc == 7))
                    return ins
                P.op("pe", mmg, reads=[R_xnT, R_ws[sg]], writes=[R_pb[kg]])
                kp_ = pb()

                def mmp(e, kp_=kp_, b=b, sp_=sp_):
                    ins = None
                    wv = wslot[sp_][:, 0:1024].rearrange("p (k n) -> p k n", k=2)
                    for pc in range(2):
                        ins = e.matmul(psb[kp_][:, :], pT[:, pc, b * 128:(b + 1) * 128], wv[:, pc, :],
                                       start=(pc == 0), stop=(pc == 1))
                    return ins
                P.op("pe", mmp, reads=[R_pT, R_ws[sp_]], writes=[R_pb[kp_]])
                P.op("act", lambda e, kg=kg, gt=gt: e.activation(gt, psb[kg][:, :], AF.Sigmoid),
                     reads=[R_pb[kg]], writes=[rg])
                pread(kg)
                P.op("dve", lambda e, kp_=kp_, gt=gt, pw=pw: e.tensor_tensor(pw, psb[kp_][:, :], gt, ALU.mult),
                     reads=[R_pb[kp_], rg], writes=[rp])
                pread(kp_)
                P.op("dve", lambda e, b=b, nh=nh, pw=pw: e.tensor_tensor(
                    h[:, b, nh * 512:(nh + 1) * 512], h[:, b, nh * 512:(nh + 1) * 512], pw, ALU.add),
                    reads=[rp, R_h[b]], writes=[R_h[b]])

    def l0_mixer(i):
        norm_stage(i)
        if DBG == 1:
            return
        (s,) = wuse(i, 0)
        for g in range(4):
            ub = arena[:, 5, :]
            tA = arena[:, 6, :]
            tB = arena[:, 7, :]
            dd = arena[:, 9, :].bitcast(BF16)[:, 0:512]
            k = pb()

            def mm(e, k=k, g=g, s=s):
                ins = None
                for kc in range(8):
                    ins = e.matmul(psb[k][:, :], w3(s)[:, kc, g * 128:(g + 1) * 128], xnT[:, kc, :],
                                   start=(kc == 0), stop=(kc == 7))
                return ins
            P.op("pe", mm, reads=[R_xnT, R_ws[s]], writes=[R_pb[k]])
            P.op("dve", lambda e, g=g: e.tensor_copy(ub[:, 0:16], uh[:, g, :]), reads=[R_uh], writes=[R_ar[5]])
            P.op("act", lambda e, k=k: e.activation(ub[:, 16:528], psb[k][:, :], AF.Copy),
                 reads=[R_pb[k]], writes=[R_ar[5]])
            pread(k)
            P.op("dve", lambda e, g=g: e.tensor_copy(uh[:, g, :], ub[:, 512:528]), reads=[R_ar[5]], writes=[R_uh])
            src, src_r = ub, R_ar[5]
            bufs = [(tA, R_ar[6]), (tB, R_ar[7])]
            sh = 1
            for step in range(g + 1):
                dst, dst_r = bufs[step % 2]
                lo = 2 * sh - 1
                P.op("dve", lambda e, src=src, dst=dst, sh=sh, lo=lo: e.tensor_tensor(
                    dst[:, lo:528], src[:, lo:528], src[:, lo - sh:528 - sh], ALU.add),
                    reads=[src_r], writes=[dst_r])
                src, src_r = dst, dst_r
                sh *= 2
            w = 2 ** (g + 1)
            P.op("dve", lambda e, src=src, w=w: e.scalar_tensor_tensor(
                dd, src[:, 16:528], 1.0 / w, ub[:, 16:528], ALU.mult, ALU.subtract),
                reads=[src_r, R_ar[5]], writes=[R_ar[9]])
            if i == 0:
                P.op("dve", lambda e, src=src, g=g: e.tensor_tensor(
                    src[:, 16:32], src[:, 16:32], cf[:, 512 + g * 16:512 + (g + 1) * 16], ALU.mult),
                    reads=[src_r, R_c], writes=[src_r])
                P.op("dve", lambda e, src=src: e.tensor_tensor(dd[:, 0:16], src[:, 16:32], ub[:, 16:32], ALU.subtract),
                     reads=[src_r, R_ar[5], R_ar[9]], writes=[R_ar[9]])
            k2 = pb()
            P.op("pe", lambda e, k2=k2, g=g: e.matmul(psb[k2][:, :], pw_sb[:, g, :], dd, start=True, stop=True),
                 reads=[R_ar[9], R_c], writes=[R_pb[k2]])
            P.op("act", lambda e, k2=k2, g=g: e.activation(abT[:, g, :], psb[k2][:, :], AF.Copy, scale=vcol(48 + g)),
                 reads=[R_pb[k2], R_c], writes=[R_ab[g]])
            pread(k2)
        if DBG == 2:
            return
        A0 = arena[:, 0, 0:512]
        A1 = arena[:, 1, 0:512]
        A2 = arena[:, 2, 0:512]
        A3 = arena[:, 3, 0:512]
        A4s = [(arena[:, 8, 0:512], R_ar[8]), (arena[:, 10, 0:512], R_ar[10])]
        QKs = [(arena[:, 4, :].bitcast(BF16), R_ar[4]), (arena[:, 11, :].bitcast(BF16), R_ar[11])]
        r5 = arena[:, 5, :].bitcast(BF16)
        r6 = arena[:, 6, :].bitcast(BF16)
        r7 = arena[:, 7, :].bitcast(BF16)
        T9 = arena[:, 9, 0:512]
        khT = r5[:, 512:1024].rearrange("p (b v) -> p b v", b=4)
        AT = r6[:, 0:512].rearrange("p (b v) -> p b v", b=4)
        osq = r6[:, 512:1024]
        Sdb = r7[:, 0:1024].rearrange("p (c v) -> p c v", c=8)

        def early(hh):
            par = hh % 2
            A4, rA4 = A4s[par]
            QK, rQK = QKs[par]
            qt, kh = QK[:, 0:512], QK[:, 512:1024]
            Ddp = Dd2[:, par, :]
            rDd = R_Dd2[par]
            (s,) = wuse(i, 1 + hh)
            kq, kf, kg = pb(), pb(), pb()
            for kk, off in ((kq, 0), (kf, 128), (kg, 256)):
                def mm(e, kk=kk, off=off, s=s):
                    ins = None
                    for kc in range(8):
                        ins = e.matmul(psb[kk][:, :], w3(s)[:, kc, off:off + 128], xnT[:, kc, :],
                                       start=(kc == 0), stop=(kc == 7))
                    return ins
                P.op("pe", mm, reads=[R_xnT, R_ws[s]], writes=[R_pb[kk]])
            ki = pb()

            def mmi(e, ki=ki, s=s):
                ins = None
                for b in range(4):
                    for kc in range(8):
                        ins = e.matmul(psb[ki][:, b * 128:(b + 1) * 128], xnT[:, kc, b * 128:(b + 1) * 128],
                                       w3(s)[:, kc, 384:512], start=(kc == 0), stop=(kc == 7))
                return ins
            P.op("pe", mmi, reads=[R_xnT, R_ws[s]], writes=[R_pb[ki]])
            P.op("act", lambda e, kf=kf: e.activation(A0, psb[kf][:, :], AF.Sigmoid),
                 reads=[R_pb[kf]], writes=[R_ar[0]])
            pread(kf)
            P.op("act", lambda e, kg=kg, A4=A4: e.activation(A4, psb[kg][:, :], AF.Sigmoid),
                 reads=[R_pb[kg]], writes=[rA4])
            P.op("act", lambda e, ki=ki, par=par: e.activation(itok2[:, par, :], psb[ki][:, :], AF.Copy),
                 reads=[R_pb[ki]], writes=[R_itok2[par]])
            pread(ki)
            P.op("dve", lambda e, kg=kg, A4=A4: e.scalar_tensor_tensor(A4, psb[kg][:, :], vcol(64), A4, ALU.mult, ALU.mult),
                 reads=[R_pb[kg], rA4, R_c], writes=[rA4])
            pread(kg)
            P.op("dve", lambda e, hh=hh: e.tensor_scalar(A0, A0, lbt[:, 16 + hh:17 + hh], lbt[:, 12 + hh:13 + hh],
                                                        ALU.mult, ALU.add),
                 reads=[R_ar[0], R_lb], writes=[R_ar[0]])
            P.op("act", lambda e: e.activation(A1, A0, AF.Ln), reads=[R_ar[0]], writes=[R_ar[1]])
            P.op("dve", lambda e: e.tensor_scalar(A0, A0, -1.0, 1.0, ALU.mult, ALU.add),
                 reads=[R_ar[0], R_ar[1]], writes=[R_ar[0]])
            P.op("dve", lambda e: e.tensor_tensor_scan(A2, scanm, A1, 0.0, ALU.mult, ALU.add),
                 reads=[R_ar[1], R_c], writes=[R_ar[2]])
            A1v = A1.rearrange("p (c j) -> p c j", j=64)
            A2v = A2.rearrange("p (c j) -> p c j", j=64)
            P.op("dve", lambda e: e.tensor_tensor(A1v, A2v, A2v[:, :, 63:64].to_broadcast([128, 8, 64]), ALU.subtract),
                 reads=[R_ar[2], R_ar[1]], writes=[R_ar[1]])
            P.op("act", lambda e, Ddp=Ddp: e.activation(Ddp, A2[:, 63:512:64], AF.Exp), reads=[R_ar[2]], writes=[rDd])
            P.op("act", lambda e: e.activation(A3, A1, AF.Exp, scale=-1.0), reads=[R_ar[1]], writes=[R_ar[3]])
            P.op("act", lambda e: e.activation(A1, A1, AF.Exp), reads=[R_ar[1], R_ar[3]], writes=[R_ar[1]])
            P.op("dve", lambda e, kq=kq, qt=qt: e.tensor_tensor(qt, psb[kq][:, :], A1, ALU.mult),
                 reads=[R_pb[kq], R_ar[1]], writes=[rQK])
            pread(kq)
            P.op("dve", lambda e, kh=kh: e.tensor_tensor(kh, A0, A3, ALU.mult),
                 reads=[R_ar[0], R_ar[3], rQK], writes=[rQK])

        def late(hh):
            par = hh % 2
            A4, rA4 = A4s[par]
            QK, rQK = QKs[par]
            qt, kh = QK[:, 0:512], QK[:, 512:1024]
            Ddp = Dd2[:, par, :]
            rDd = R_Dd2[par]
            itokh = itok2[:, par, :].rearrange("p (b v) -> p b v", b=4)
            rIt = R_itok2[par]
            kt = pb()

            def trk(e, kt=kt, kh=kh):
                ins = None
                for b in range(4):
                    ins = e.transpose(ps_bf(kt)[:, b * 128:(b + 1) * 128], kh[:, b * 128:(b + 1) * 128], ident)
                return ins
            P.op("pe", trk, reads=[rQK, R_c], writes=[R_pb[kt]], cost=0.5)
            P.op("act", lambda e, kt=kt: e.activation(r5[:, 512:1024], ps_bf(kt)[:, 0:512], AF.Copy),
                 reads=[R_pb[kt]], writes=[R_ar[5]])
            pread(kt)
            ks = pb()

            def sc(e, ks=ks, kh=kh, qt=qt):
                ins = None
                for b in range(4):
                    ins = e.matmul(psb[ks][:, b * 128:(b + 1) * 128], kh[:, b * 128:(b + 1) * 128],
                                   qt[:, b * 128:(b + 1) * 128], start=True, stop=True)
                return ins
            P.op("pe", sc, reads=[rQK], writes=[R_pb[ks]], cost=0.5)
            P.op("dve", lambda e, ks=ks: e.tensor_tensor(
                AT, psb[ks][:, :].rearrange("p (b v) -> p b v", b=4),
                hmask.unsqueeze(1).to_broadcast([128, 4, 128]), ALU.mult),
                reads=[R_pb[ks], R_c], writes=[R_ar[6]])
            pread(ks)
            ku = [pb(), pb()]

            def um(e, ku=ku, itokh=itokh):
                ins = None
                for c in range(8):
                    b, pr_ = c // 2, c % 2
                    ins = e.matmul(psb[ku[pr_]][:, b * 128:(b + 1) * 128],
                                   khT[64 * pr_:64 * pr_ + 64, b, :], itokh[64 * pr_:64 * pr_ + 64, b, :],
                                   start=True, stop=True)
                return ins
            P.op("pe", um, reads=[R_ar[5], rIt], writes=[R_pb[ku[0]], R_pb[ku[1]]], cost=0.6)
            for c in range(8):
                P.op("dve", lambda e, c=c, hh=hh, Ddp=Ddp: e.tensor_scalar(Sdb[:, c, :], S32[:, hh, :], Ddp[:, c:c + 1], None, ALU.mult),
                     reads=[R_S[hh], rDd], writes=[R_ar[7]], cost=0.3)
                P.op("dve", lambda e, c=c, hh=hh, ku=ku, Ddp=Ddp: e.scalar_tensor_tensor(
                    S32[:, hh, :], S32[:, hh, :], Ddp[:, c:c + 1], psb[ku[c % 2]][:, (c // 2) * 128:(c // 2 + 1) * 128],
                    ALU.mult, ALU.add),
                    reads=[R_S[hh], rDd, R_pb[ku[c % 2]]], writes=[R_S[hh]], cost=0.35)
            pread(ku[0])
            pread(ku[1])
            ko = pb()

            def om(e, ko=ko, qt=qt, itokh=itokh):
                ins = None
                first = True
                for b in range(4):
                    for c in (2 * b, 2 * b + 1):
                        ins = e.matmul(psb[ko][:, c * 64:(c + 1) * 64], Sdb[:, c, :], qt[:, c * 64:(c + 1) * 64],
                                       start=first, stop=False)
                        first = False
                    ins = e.matmul(psb[ko][:, b * 128:(b + 1) * 128], itokh[:, b, :], AT[:, b, :],
                                   start=False, stop=(b == 3))
                return ins
            P.op("pe", om, reads=[R_ar[7], rQK, rIt, R_ar[6]], writes=[R_pb[ko]], cost=0.9)
            P.op("act", lambda e, ko=ko: e.activation(osq, psb[ko][:, :], AF.Square),
                 reads=[R_pb[ko], R_ar[6]], writes=[R_ar[6]])
            kn = pb()
            P.op("pe", lambda e, kn=kn: e.matmul(psb[kn][:, :], ones_b, osq, start=True, stop=True),
                 reads=[R_ar[6], R_c], writes=[R_pb[kn]])
            P.op("act", lambda e, kn=kn: e.activation(T9, psb[kn][:, :], AF.Ln, bias=eps_ap, scale=1.0 / 128),
                 reads=[R_pb[kn], R_c], writes=[R_ar[9]])
            pread(kn)
            P.op("act", lambda e: e.activation(T9, T9, AF.Exp, scale=-0.5), reads=[R_ar[9]], writes=[R_ar[9]])
            P.op("dve", lambda e, ko=ko: e.tensor_tensor(T9, psb[ko][:, :], T9, ALU.mult),
                 reads=[R_pb[ko], R_ar[9]], writes=[R_ar[9]])
            pread(ko)
            P.op("dve", lambda e, hh=hh, A4=A4: e.tensor_tensor(abT[:, 4 + hh, :], T9, A4, ALU.mult),
                 reads=[R_ar[9], rA4], writes=[R_ab[4 + hh]])

        early(0)
        for hh in range(4):
            if hh + 1 < 4:
                early(hh + 1)
            late(hh)
        if DBG == 3:
            return
        tokmajor_proj(i, 5, abT, R_ab, 0)
        tokmajor_proj(i, 6, abT, R_ab, 1)

    def l1_mixer(i):
        norm_stage(i)
        sl = i % 5
        P.op("pool", lambda e: e.dma_start(out=cosb[:], in_=cos_d[:, i * T:(i + 1) * T]), writes=[R_rope], chan="rope")
        P.op("pool", lambda e: e.dma_start(out=sinb[:], in_=sin_d[:, i * T:(i + 1) * T]), writes=[R_rope], chan="rope")
        X = arena[:, 0, :].bitcast(BF16)
        zb, sq = X[:, 0:512], X[:, 512:1024]
        Y = arena[:, 1, 0:512]
        Z = arena[:, 2, 0:512]
        W = arena[:, 3, 0:512]
        natacc = arena[:, 0:4, 0:512]
        QNr, QPr = (4, 12), (5, 13)

        def qpad(regs, j, hp):
            return arena[:, regs[j], :].bitcast(BF16)[:, hp * 512:(hp + 1) * 512]
        for r_ in (4, 5, 12, 13):
            P.op("pool", lambda e, r_=r_: e.memset(arena[:, r_, :], 0.0), writes=[R_ar[r_]], cost=1.0)
        PT = [arena[:, 6, :].bitcast(BF16)[:, 0:512], arena[:, 6, :].bitcast(BF16)[:, 512:1024],
              arena[:, 7, :].bitcast(BF16)[:, 0:512], arena[:, 7, :].bitcast(BF16)[:, 512:1024]]
        PT_res = [R_ar[6], R_ar[6], R_ar[7], R_ar[7]]
        XS = [(arena[:, 0, :].bitcast(BF16), arena[:, 1, 0:512], arena[:, 2, 0:512], arena[:, 3, 0:512], R_ar[0], R_ar[1], R_ar[2], R_ar[3]),
              (arena[:, 8, :].bitcast(BF16), arena[:, 9, 0:512], arena[:, 10, 0:512], arena[:, 11, 0:512], R_ar[8], R_ar[9], R_ar[10], R_ar[11])]
        for g in range(4):
            (s,) = wuse(i, 27 + 2 * g)
            st1 = {}

            def stage1(wi, g=g, s=s):
                Xb, Y_, Z_, W_, rX, rY, rZ, rW = XS[wi % 2]
                zb_, sq_ = Xb[:, 0:512], Xb[:, 512:1024]
                col0 = wi * 128
                kz = pb()

                def mm(e, kz=kz, col0=col0, s=s):
                    ins = None
                    for kc in range(8):
                        ins = e.matmul(psb[kz][:, :], w3(s)[:, kc, col0:col0 + 128], xnT[:, kc, :],
                                       start=(kc == 0), stop=(kc == 7))
                    return ins
                P.op("pe", mm, reads=[R_xnT, R_ws[s]], writes=[R_pb[kz]])
                P.op("act", lambda e, kz=kz, zb_=zb_: e.activation(zb_, psb[kz][:, :], AF.Copy), reads=[R_pb[kz]], writes=[rX])
                P.op("act", lambda e, kz=kz, sq_=sq_: e.activation(sq_, psb[kz][:, :], AF.Square), reads=[R_pb[kz]], writes=[rX])
                st1[wi] = kz

            def stage2(wi, g=g):
                Xb, Y_, Z_, W_, rX, rY, rZ, rW = XS[wi % 2]
                zb_, sq_ = Xb[:, 0:512], Xb[:, 512:1024]
                isk = wi >= 2
                j = wi % 2
                kz = st1[wi]
                kr, kss = pb(), pb()
                P.op("pe", lambda e, kr=kr, zb_=zb_: e.matmul(psb[kr][:, :], prot, zb_, start=True, stop=True),
                     reads=[rX, R_c], writes=[R_pb[kr]], cost=0.3)
                P.op("pe", lambda e, kss=kss, sq_=sq_: e.matmul(psb[kss][:, :], bd64, sq_, start=True, stop=True),
                     reads=[rX, R_c], writes=[R_pb[kss]], cost=0.3)
                gc = 67 if isk else 65
                P.op("dve", lambda e, kz=kz, gc=gc, Y_=Y_: e.scalar_tensor_tensor(Y_, psb[kz][:, :], vcol(gc), cosb[:], ALU.mult, ALU.mult),
                     reads=[R_pb[kz], R_rope, R_c], writes=[rY])
                pread(kz)
                P.op("dve", lambda e, kr=kr, gc=gc, Z_=Z_: e.scalar_tensor_tensor(Z_, psb[kr][:, :], vcol(gc + 1), sinb[:], ALU.mult, ALU.mult),
                     reads=[R_pb[kr], R_rope, R_c], writes=[rZ])
                pread(kr)
                P.op("act", lambda e, kss=kss, W_=W_: e.activation(W_, psb[kss][:, :], AF.Ln, bias=eps_ap, scale=1.0 / 64),
                     reads=[R_pb[kss], R_c], writes=[rW])
                pread(kss)
                P.op("act", lambda e, W_=W_: e.activation(W_, W_, AF.Exp, scale=-0.5), reads=[rW], writes=[rW])
                P.op("dve", lambda e, Y_=Y_, Z_=Z_: e.tensor_tensor(Y_, Y_, Z_, ALU.add), reads=[rY, rZ], writes=[rY])
                if not isk:
                    for hp in range(2):
                        rows = slice(64 * hp, 64 * hp + 64)
                        dn, dn_r = qpad(QNr, j, hp), R_ar[QNr[j]]
                        dp, dp_r = qpad(QPr, j, hp), R_ar[QPr[j]]
                        P.op("dve", lambda e, dn=dn, Y_=Y_, W_=W_, rows=rows: e.tensor_tensor(dn[rows, :], Y_[rows, :], W_[rows, :], ALU.mult),
                             reads=[rY, rW], writes=[dn_r])
                        P.op("pool", lambda e, dn=dn, dp=dp, rows=rows: e.tensor_copy(
                            dp[rows, :].rearrange("p (r j) -> p r j", r=4), dn[rows, :].rearrange("p (j r) -> p r j", r=4)),
                            reads=[dn_r], writes=[dp_r], cost=1.6)
                else:
                    dn, dn_r = Knat[:, 2 * g + j, 128:128 + T], R_KN[g]
                    dp, dp_r = Kperm[:, 2 * g + j, sl * T:(sl + 1) * T], R_KP[sl][g]
                    P.op("dve", lambda e, dn=dn, Y_=Y_, W_=W_: e.tensor_tensor(dn, Y_, W_, ALU.mult),
                         reads=[rY, rW], writes=[dn_r])
                    P.op("pool", lambda e, dn=dn, dp=dp: e.tensor_copy(
                        dp.rearrange("p (r j) -> p r j", r=4), dn.rearrange("p (j r) -> p r j", r=4)),
                        reads=[dn_r], writes=[dp_r], cost=1.6)

            stage1(0)
            for wi in range(4):
                if wi + 1 < 4:
                    stage1(wi + 1)
                stage2(wi)
            if DBG in (20, 30, 31, 32, 33, 34, 35):
                for _k in range(8):
                    pread(_k)
                continue
            (sv,) = wuse(i, 28 + 2 * g)
            wv = wslot[sv][:, 0:2048].rearrange("p (k n) -> p k n", k=8)
            for half in range(2):
                for perm in (False, True):
                    kv = pb()

                    def mmv(e, kv=kv, half=half, perm=perm, wv=wv):
                        ins = None
                        for bb in range(2):
                            b = half * 2 + bb
                            for kc in range(8):
                                lhs = xnT[:, kc, b:T:4] if perm else xnT[:, kc, b * 128:(b + 1) * 128]
                                ins = e.matmul(psb[kv][:, bb * 256:(bb + 1) * 256], lhs, wv[:, kc, :],
                                               start=(kc == 0), stop=(kc == 7))
                        return ins
                    P.op("pe", mmv, reads=[R_xnT, R_ws[sv]], writes=[R_pb[kv]])
                    if perm:
                        dst = Vperm[:, sl * 4 + half * 2:sl * 4 + half * 2 + 2, 256 * g:256 * g + 256]
                        dr = R_VP[sl][g]
                    else:
                        dst = Vnat[:, 1 + half * 2:3 + half * 2, 256 * g:256 * g + 256]
                        dr = R_VN[g]
                    P.op("act", lambda e, kv=kv, dst=dst: e.activation(
                        dst, psb[kv][:, :].rearrange("p (b n) -> p b n", b=2), AF.Copy),
                        reads=[R_pb[kv]], writes=[dr])
                    pread(kv)
            if DBG == 21:
                for _k in range(8):
                    pread(_k)
                continue
            pairs = []
            acci = 0
            for qb in range(4):
                lst = []
                if not (i == 0 and qb == 0):
                    lst.append((lambda j, qb=qb, g=g: Knat[:, 2 * g + j, qb * 128:(qb + 1) * 128], Vnat[:, qb, :], m1p,
                                [R_KN[g], R_VN[g]]))
                lst.append((lambda j, qb=qb, g=g: Knat[:, 2 * g + j, (qb + 1) * 128:(qb + 2) * 128], Vnat[:, qb + 1, :], m1c,
                            [R_KN[g], R_VN[g]]))
                for n, (kf_, v_, m_, rr_) in enumerate(lst):
                    pairs.append(dict(q=(QNr, qb, [R_ar[4], R_ar[12]]), kf=kf_, v=v_, m=m_, rr=rr_, acc=6 + acci % 2,
                                      first=(n == 0), last=(n == len(lst) - 1), nat=True, idx=qb))
                acci += 1
            for c in range(4):
                lst = []
                for mback, msk in ((4, m16f), (3, m16m), (2, m16m), (1, mpp), (0, mpc)):
                    if i - mback < 0:
                        continue
                    s2 = (i - mback) % 5
                    lst.append((lambda j, s2=s2, c=c, g=g: Kperm[:, 2 * g + j, s2 * T + c * 128:s2 * T + (c + 1) * 128],
                                Vperm[:, s2 * 4 + c, :], msk, [R_KP[s2][g], R_VP[s2][g]]))
                for n, (kf_, v_, m_, rr_) in enumerate(lst):
                    pairs.append(dict(q=(QPr, c, [R_ar[5], R_ar[13]]), kf=kf_, v=v_, m=m_, rr=rr_, acc=6 + acci % 2,
                                      first=(n == 0), last=(n == len(lst) - 1), nat=False, idx=c))
                acci += 1

            def emit_scores(pr, n):
                kx = pb()
                qregs, qidx, qres = pr["q"]

                def scm(e, kx=kx, pr=pr, qregs=qregs, qidx=qidx):
                    ins = None
                    for hd in range(4):
                        j, hp = hd // 2, hd % 2
                        ins = e.matmul(psb[kx][:, hd * 128:(hd + 1) * 128],
                                       pr["kf"](j), qpad(qregs, j, hp)[:, qidx * 128:(qidx + 1) * 128],
                                       start=True, stop=True)
                    return ins
                P.op("pe", scm, reads=qres + pr["rr"], writes=[R_pb[kx]], cost=0.35)
                pr["ksc"] = kx
                pr["pt"] = n % 4

            def emit_softmax(pr):
                kx, ptb_, pres = pr["ksc"], PT[pr["pt"]], PT_res[pr["pt"]]
                P.op("act", lambda e, kx=kx, ptb_=ptb_: e.activation(ptb_, psb[kx][:, :], AF.Exp, scale=0.125),
                     reads=[R_pb[kx]], writes=[pres], cost=0.65)
                pread(kx)
                P.op("dve", lambda e, ptb_=ptb_, pr=pr: e.tensor_tensor(
                    ptb_.rearrange("p (h q) -> p h q", h=4), ptb_.rearrange("p (h q) -> p h q", h=4),
                    pr["m"].unsqueeze(1).to_broadcast([128, 4, 128]), ALU.mult),
                    reads=[pres, R_c], writes=[pres], cost=0.45)

            def emit_pv(pr):
                ptb_, pres, ka = PT[pr["pt"]], PT_res[pr["pt"]], pr["acc"]

                def pvm(e, ptb_=ptb_, pr=pr, ka=ka, g=g):
                    ins = None
                    for hd in range(4):
                        j, hp = hd // 2, hd % 2
                        rows = slice(64 * hp, 64 * hp + 64)
                        ins = e.matmul(psb[ka][rows, j * 128:(j + 1) * 128],
                                       pr["v"][:, 256 * g + 64 * hd:256 * g + 64 * hd + 64],
                                       ptb_[:, hd * 128:(hd + 1) * 128],
                                       start=(pr["first"] and j == 0), stop=False)
                        ins = e.matmul(psb[ka][rows, (2 + j) * 128:(3 + j) * 128],
                                       ones_b[:, 0:64], ptb_[:, hd * 128:(hd + 1) * 128],
                                       start=False, stop=(pr["last"] and j == 1))
                    return ins
                P.op("pe", pvm, reads=[pres, R_c] + pr["rr"], writes=[R_pb[ka]], cost=0.7)
                if pr["last"]:
                    if pr["nat"]:
                        qb = pr["idx"]
                        P.op("act", lambda e, ka=ka, qb=qb: e.activation(
                            natacc[:, :, qb * 128:(qb + 1) * 128], psb[ka][:, :].rearrange("p (k q) -> p k q", k=4), AF.Copy),
                            reads=[R_pb[ka]], writes=R_ar[0:4])
                    else:
                        c = pr["idx"]
                        P.op("dve", lambda e, ka=ka, c=c: e.tensor_tensor(
                            tot[:], psb[ka][:, :].rearrange("p (k q) -> p k q", k=4), natacc[:, :, c:T:4], ALU.add),
                            reads=[R_pb[ka]] + R_ar[0:4], writes=[R_tot])
                        P.op("dve", lambda e: e.reciprocal(tot[:, 2:4, :], tot[:, 2:4, :]), reads=[R_tot], writes=[R_tot])
                        P.op("dve", lambda e, c=c, g=g: e.tensor_tensor(
                            abT[:, 2 * g:2 * g + 2, c:T:4], tot[:, 0:2, :], tot[:, 2:4, :], ALU.mult),
                            reads=[R_tot], writes=[R_ab[2 * g], R_ab[2 * g + 1]])

            for n, pr in enumerate(pairs):
                emit_scores(pr, n)
                emit_softmax(pr)
                if n >= 3:
                    emit_pv(pairs[n - 3])
            for pr_ in pairs[-3:]:
                emit_pv(pr_)
            P.op("pool", lambda e, g=g: e.tensor_copy(Knat[:, 2 * g:2 * g + 2, 0:128], Knat[:, 2 * g:2 * g + 2, T:T + 128]),
                 reads=[R_KN[g]], writes=[R_KN[g]])
            P.op("pool", lambda e, g=g: e.tensor_copy(Vnat[:, 0, 256 * g:256 * g + 256], Vnat[:, 4, 256 * g:256 * g + 256]),
                 reads=[R_VN[g]], writes=[R_VN[g]])
        if DBG == 40:
            P.op("dve", lambda e: e.tensor_copy(h[:].rearrange("p b d -> p (b d)"), abT[:].rearrange("p c t -> p (c t)")),
                 reads=R_ab, writes=R_h)
            return
        if DBG == 41:
            P.op("dve", lambda e: e.tensor_copy(h[:].rearrange("p b d -> p (b d)"), Knat[:, :, 128:640].rearrange("p c t -> p (c t)")),
                 reads=R_KN, writes=R_h)
            return
        if DBG == 42:
            P.op("dve", lambda e: e.tensor_copy(h[:].rearrange("p b d -> p (b d)"), Vnat[:, 1:5, :].rearrange("p c t -> p (c t)")),
                 reads=R_VN, writes=R_h)
            return
        tokmajor_proj(i, 35, abT, R_ab, 0)
        tokmajor_proj(i, 36, abT, R_ab, 1)


    for i in range(nt):
        for b in range(4):
            P.op("pool", lambda e, b=b, i=i: e.dma_start(out=h[:, b, :], in_=x_d[i * T + b * 128:i * T + (b + 1) * 128, :]),
                 writes=[R_h[b]], chan=f"xin{b}")
        if upto >= 1:
            P.tag = f"{i}:l0mix"
            l0_mixer(i)
        if upto >= 2:
            P.tag = f"{i}:mlp0"
            mlp_stage(i, 7)
        if upto >= 3:
            P.tag = f"{i}:ple0"
            ple_stage(i, 23, 0)
        if upto >= 4:
            P.tag = f"{i}:l1mix"
            l1_mixer(i)
        if upto >= 5:
            P.tag = f"{i}:mlp1"
            mlp_stage(i, 37)
        if upto >= 6:
            P.tag = f"{i}:ple1"
            ple_stage(i, 53, 1)
        P.tag = f"{i}:io"
        for b in range(4):
            P.op("pool", lambda e, b=b, i=i: e.dma_start(out=out_d[i * T + b * 128:i * T + (b + 1) * 128, :], in_=h[:, b, :]),
                 reads=[R_h[b]], chan=f"out{b}")

    sems = {k: es.enter_context(nc.semaphore("s_" + k)) for k in ("pe", "act", "dve", "pool")}
    if SCHED:
        P.schedule()
        print('sched sim_time us', P.sim_time, 'nops', len(P.ops), flush=True)
    else:
        P.by_eng = {e: [x for x in P.ops if x.eng == e] for e in ENGS}
    chan_names = sorted({x.chan for x in P.ops if x.chan is not None})
    chs = {k: es.enter_context(nc.semaphore("c_" + k)) for k in chan_names}
    with nc.Block() as block:
        P.emit(block, sems, chs, final_waits=[f"out{b}" for b in range(4)])
    es.close()
    return nc


def run(inputs, nt=SEQ // T, upto=6, n_cores=8, trace=False):
    S = nt * T
    ws = _pack_weights(inputs)
    vecs = _pack_vec(inputs)
    cbf, cf, cosd, sind = _consts(S)
    pool_w = np.ascontiguousarray(inputs["pool_w"][0])
    nc = build(nt, upto)
    x, p = inputs["x"], inputs["p"]
    in_maps = []
    for c in range(n_cores):
        in_maps.append({
            "x": np.ascontiguousarray(x[c, :S]), "p": np.ascontiguousarray(p[:, c, :S]),
            "wsrc": ws, "vec": vecs, "cbf": cbf, "cf": cf, "cosd": cosd, "sind": sind, "pool_w": pool_w,
        })
    res = run_bass_kernel_spmd(nc, in_maps, core_ids=list(range(n_cores)), trace=trace)
    out = np.stack([res.results[c]["out"] for c in range(n_cores)], axis=0)
    return out, res


def kernel(**inputs):
    inputs = {k: np.asarray(v) for k, v in inputs.items()}
    out, _ = run(inputs)
    return out.astype(np.float32)
```

```python
import numpy as np
from contextlib import ExitStack
import concourse.bass as bass
import concourse.mybir as mybir
from concourse.bass_utils import run_bass_kernel_spmd

F32 = mybir.dt.float32
BF16 = mybir.dt.bfloat16
AF = mybir.ActivationFunctionType
ALU = mybir.AluOpType

D = 1024
SEQ = 8192
T = 512
NJ = 57
SLOT = 4096
EPS = 1e-6
ENGS = ("pe", "act", "dve", "pool", "sp")
DEF_COST = {"pe": 2.2, "act": 0.65, "dve": 0.7, "pool": 1.2, "sp": 4.0}
SYNC_LAT = 0.15
SCHED = True
PRIO = "cp"
DBG = 0


class Res:
    __slots__ = ("name", "w", "rs", "const", "excl")

    def __init__(self, name, const=False, excl=False):
        self.name = name
        self.w = None
        self.rs = []
        self.const = const
        self.excl = excl


class Op:
    __slots__ = ("eng", "fn", "deps", "marked", "count", "chan", "chan_count", "cost", "idx", "tag", "st", "ft")


class Prog:
    def __init__(self):
        self.ops = []
        self.by_eng = {e: [] for e in ENGS}
        self.chan_counts = {}

    def op(self, eng, fn, reads=(), writes=(), chan=None, cost=None):
        x = Op()
        x.eng = eng
        if cost is None:
            cost = 4.0 if chan is not None else DEF_COST[eng]
        x.cost = cost
        x.idx = len(self.ops)
        x.tag = getattr(self, "tag", "")
        x.fn = fn
        x.marked = False
        x.count = 0
        x.chan = chan
        x.chan_count = 0
        deps = []
        seen = set()

        def add(y):
            if y is not None and id(y) not in seen:
                seen.add(id(y))
                deps.append(y)

        writes = list(writes) + [r for r in reads if r.excl]
        reads = [r for r in reads if not r.excl]
        for r in reads:
            add(r.w)
        for w in writes:
            add(w.w)
            for y in w.rs:
                add(y)
        for r in reads:
            if not r.const:
                r.rs.append(x)
        for w in writes:
            w.w = x
            w.rs = []
        x.deps = deps
        self.ops.append(x)
        return x

    def schedule(self):
        import heapq
        ops = self.ops
        n = len(ops)
        succ = [[] for _ in range(n)]
        indeg = [0] * n
        for x in ops:
            indeg[x.idx] = len(x.deps)
            for y in x.deps:
                succ[y.idx].append(x.idx)
        bl = [0.0] * n
        for i in range(n - 1, -1, -1):
            m = 0.0
            for j in succ[i]:
                if bl[j] > m:
                    m = bl[j]
            bl[i] = ops[i].cost + m + (SYNC_LAT if succ[i] else 0.0)
        if PRIO == "cp":
            prio = [-b for b in bl]
        else:
            prio = list(range(n))
        ready_t = [0.0] * n
        fin = [0.0] * n
        free = {e: 0.0 for e in ENGS}
        pend = {e: [] for e in ENGS}
        avail = {e: [] for e in ENGS}
        for x in ops:
            if indeg[x.idx] == 0:
                heapq.heappush(pend[x.eng], (0.0, prio[x.idx], x.idx))
        order = {e: [] for e in ENGS}
        done = 0
        while done < n:
            best = None
            for e in ENGS:
                pe_, av = pend[e], avail[e]
                while pe_ and pe_[0][0] <= free[e]:
                    t_ = heapq.heappop(pe_)
                    heapq.heappush(av, (t_[1], t_[2]))
                if av:
                    cand = (free[e], av[0][0], e, True)
                elif pe_:
                    cand = (pe_[0][0], pe_[0][1], e, False)
                else:
                    continue
                if best is None or cand[:2] < best[:2]:
                    best = cand
            st, _p, e, from_av = best
            if from_av:
                i = heapq.heappop(avail[e])[1]
            else:
                i = heapq.heappop(pend[e])[2]
            x = ops[i]
            if x.chan is not None:
                free[e] = st + 0.15
                fin[i] = st + x.cost
            else:
                free[e] = st + x.cost
                fin[i] = free[e]
            order[e].append(x)
            x.st, x.ft = st, fin[i]
            done += 1
            for j in succ[i]:
                if fin[i] > ready_t[j]:
                    ready_t[j] = fin[i]
                indeg[j] -= 1
                if indeg[j] == 0:
                    heapq.heappush(pend[ops[j].eng], (ready_t[j] + SYNC_LAT, prio[j], j))
        self.by_eng = order
        self.sim_time = max(fin) if n else 0.0

    def assign_chans(self):
        self.chan_counts = {}
        allops = []
        for e in ENGS:
            allops.extend(self.by_eng[e])
        for e in ENGS:
            for x in self.by_eng[e]:
                if x.chan is not None:
                    self.chan_counts[x.chan] = self.chan_counts.get(x.chan, 0) + 16
                    x.chan_count = self.chan_counts[x.chan]

    def finalize(self):
        for x in self.ops:
            for y in x.deps:
                if y.chan is not None:
                    continue
                if y.eng == "pe" and x.eng == "pe":
                    continue
                y.marked = True
        for e in ENGS:
            c = 0
            for x in self.by_eng[e]:
                if x.marked:
                    c += 1
                    x.count = c

    def emit(self, block, sems, chan_sems, final_waits=()):
        self.finalize()
        self.assign_chans()
        engs = {"pe": block.tensor, "act": block.scalar, "dve": block.vector,
                "pool": block.gpsimd, "sp": block.sync}
        prog = self

        def make(ename):
            def body(e):
                waited = {}
                for x in prog.by_eng[ename]:
                    for y in x.deps:
                        if y.chan is not None:
                            key, val, sem = "c:" + y.chan, y.chan_count, chan_sems[y.chan]
                        else:
                            if y.eng == "pe" and ename == "pe":
                                continue
                            key, val, sem = y.eng, y.count, sems[y.eng]
                        if waited.get(key, 0) < val:
                            e.wait_ge(sem, val)
                            waited[key] = val
                    ins = x.fn(e)
                    if x.chan is not None:
                        ins.then_inc(chan_sems[x.chan], 16)
                    elif x.marked:
                        ins.then_inc(sems[ename], 1)
                if ename == "sp":
                    for ch in final_waits:
                        e.wait_ge(chan_sems[ch], prog.chan_counts[ch])
            return body

        for ename in ENGS:
            engs[ename](make(ename))


def _slotify(w, rows, c0, nc_):
    out = np.empty((128, len(rows), nc_), np.float32)
    for k, r0 in enumerate(rows):
        out[:, k, :] = w[r0:r0 + 128, c0:c0 + nc_]
    return out.reshape(128, len(rows) * nc_)


def _pack_weights(inp):
    ws = np.zeros((NJ, 128, SLOT), np.float32)
    r8 = [k * 128 for k in range(8)]
    w_in = inp["w_in_ab"][0]
    ws[0] = _slotify(w_in, r8, 0, 512)
    for hh in range(4):
        t = np.empty((128, 8, 512), np.float32)
        for k in range(8):
            rows = slice(k * 128, k * 128 + 128)
            t[:, k, 0:128] = w_in[rows, 512 + hh * 128:512 + hh * 128 + 128]
            t[:, k, 128:256] = w_in[rows, 1024 + hh * 128:1024 + hh * 128 + 128]
            t[:, k, 256:384] = w_in[rows, 2048 + hh * 128:2048 + hh * 128 + 128]
            t[:, k, 384:512] = w_in[rows, 1536 + hh * 128:1536 + hh * 128 + 128]
        ws[1 + hh] = t.reshape(128, SLOT)
    w_out = inp["w_out_ab"][0]
    ws[5] = _slotify(w_out, r8, 0, 512)
    ws[6] = _slotify(w_out, r8, 512, 512)

    def mlp_ple(base, l):
        wu, wd = inp["w_up"][l], inp["w_down"][l]
        j = base
        for qd in range(4):
            for hf in range(2):
                ws[j] = _slotify(wu, r8, qd * 1024 + hf * 512, 512)
                j += 1
            for nh in range(2):
                ws[j] = _slotify(wd, [qd * 1024 + k * 128 for k in range(8)], nh * 512, 512)
                j += 1
        wp, wg = inp["w_ple"][l], inp["w_ple_gate"][l]
        for nh in range(2):
            ws[j][:, 0:1024] = _slotify(wp, [0, 128], nh * 512, 512)
            j += 1
            ws[j] = _slotify(wg, r8, nh * 512, 512)
            j += 1
        return j

    j = mlp_ple(7, 0)
    assert j == 27
    wqkv = inp["w_qkv"][0]
    for g in range(4):
        t = np.empty((128, 8, 512), np.float32)
        for k in range(8):
            rows = slice(k * 128, k * 128 + 128)
            t[:, k, 0:256] = wqkv[rows, 256 * g:256 * g + 256]
            t[:, k, 256:512] = wqkv[rows, 1024 + 256 * g:1024 + 256 * g + 256]
        ws[27 + 2 * g] = t.reshape(128, SLOT)
        ws[28 + 2 * g][:, 0:2048] = _slotify(wqkv, r8, 2048 + 256 * g, 256)
    wo = inp["w_o"][0]
    ws[35] = _slotify(wo, r8, 0, 512)
    ws[36] = _slotify(wo, r8, 512, 512)
    j = mlp_ple(37, 1)
    assert j == NJ
    return ws


def _col8(v):
    return np.ascontiguousarray(v.reshape(8, 128).T)


def _pack_vec(inp):
    cols = []
    for nm, l in (("mix_norm", 0), ("mlp_norm", 0), ("ple_norm", 0),
                  ("mix_norm", 1), ("mlp_norm", 1), ("ple_norm", 1)):
        cols.append(_col8(inp[nm][l]))
    cols.append(np.ascontiguousarray(inp["pool_scale"][0].reshape(4, 128).T))
    lb = inp["hgrn_lb"]
    cols.append(np.ascontiguousarray(lb.reshape(3, 4, 128).transpose(2, 0, 1).reshape(128, 12)))
    cols.append(inp["hgrn_o_norm"][0].reshape(128, 1))
    idx = np.arange(128) % 64
    sw = (idx + 32) % 64
    qn, kn = inp["q_norm"][0], inp["k_norm"][0]
    cols.append(np.stack([qn[idx], qn[sw], kn[idx], kn[sw]], axis=1))
    return np.ascontiguousarray(np.concatenate(cols, axis=1).astype(np.float32))


NV = 69
NCB = 11 * 128
NCF = 512 + 64 + 1


def _consts(seq):
    a = np.arange(128)
    kp, qi = a[:, None], a[None, :]
    ident = (kp == qi)
    ones = np.ones((128, 128))
    bd64 = (kp // 64 == qi // 64)
    prot = np.zeros((128, 128))
    for m in range(128):
        if m % 64 < 32:
            prot[m + 32, m] = -1.0
        else:
            prot[m - 32, m] = 1.0
    hmask = (kp <= qi) & (kp // 64 == qi // 64)
    m1p = kp >= qi
    m1c = kp <= qi
    same = ((kp - qi) % 4 == 0)
    m16f = same & (kp >= qi)
    m16m = same
    m16c = same & (kp <= qi)
    mpp = m1p.astype(np.float32) + m16m
    mpc = m1c.astype(np.float32) + m16c
    cbf = np.concatenate([np.asarray(x, np.float32) for x in
                          (ident, ones, bd64, prot, hmask, m1p, m1c, m16f, m16m, mpp, mpc)], axis=1)
    scanm = np.ones((128, 512), np.float32)
    scanm[:, ::64] = 0.0
    invc = np.zeros((128, 64), np.float32)
    for g, w in enumerate((2, 4, 8, 16)):
        invc[:, g * 16:(g + 1) * 16] = 1.0 / np.minimum(np.arange(16) + 1, w)
    cf = np.concatenate([scanm, invc, np.full((128, 1), EPS, np.float32)], axis=1)
    half = 32
    inv = (10000.0 ** (-np.arange(half, dtype=np.float32) / half)).astype(np.float32)
    pos = np.arange(seq, dtype=np.float32)
    ang = (pos[None, :] * inv[(np.arange(128) % 64) % 32][:, None]).astype(np.float32)
    return (np.ascontiguousarray(cbf.astype(np.float32)), np.ascontiguousarray(cf),
            np.cos(ang).astype(np.float32), np.sin(ang).astype(np.float32))


def build(nt, upto=6):
    S = nt * T
    nc = bass.Bass("TRN2", target_bir_lowering=False)
    P = Prog()
    es = ExitStack()

    def dram(name, shape, dt, kind):
        return nc.dram_tensor(name, shape, dt, kind=kind).ap()

    global LAST_PROG
    LAST_PROG = P
    x_d = dram("x", [S, D], F32, "ExternalInput")
    p_d = dram("p", [2, S, 256], F32, "ExternalInput")
    wsrc = dram("wsrc", [NJ, 128, SLOT], F32, "ExternalInput")
    vec_d = dram("vec", [128, NV], F32, "ExternalInput")
    cbf_d = dram("cbf", [128, NCB], F32, "ExternalInput")
    cf_d = dram("cf", [128, NCF], F32, "ExternalInput")
    cos_d = dram("cosd", [128, S], F32, "ExternalInput")
    sin_d = dram("sind", [128, S], F32, "ExternalInput")
    out_d = dram("out", [S, D], F32, "ExternalOutput")
    wsc = dram("wsc", [NJ, 128, SLOT], BF16, "Internal")

    def sb(name, shape, dt):
        return es.enter_context(nc.sbuf_tensor(name, shape, dt))

    h = sb("h", [128, 4, D], F32)
    xtok2 = [sb(f"xtok{q}", [128, D], BF16) for q in range(2)]
    xnT = sb("xnT", [128, 8, T], BF16)
    abT = sb("abT", [128, 8, T], BF16)
    arena = sb("arena", [128, 14, 528], F32)
    wslot = [sb(f"wslot{i}", [128, SLOT], BF16) for i in range(3)]
    Kperm = sb("Kperm", [128, 8, 5 * T], BF16)
    Vperm = sb("Vperm", [128, 20, D], BF16)
    Knat = sb("Knat", [128, 8, 128 + T], BF16)
    Vnat = sb("Vnat", [128, 5, D], BF16)
    cosb = sb("cosb", [128, T], F32)
    sinb = sb("sinb", [128, T], F32)
    ptok = arena[:, 6, :].bitcast(BF16)[:, 0:1024].rearrange("p (b f) -> p b f", b=4)
    pT = arena[:, 7, :].bitcast(BF16)[:, 0:1024].rearrange("p (c t) -> p c t", c=2)
    cbf = sb("cbf_sb", [128, NCB], BF16)
    cf = sb("cf_sb", [128, NCF], F32)
    vec = sb("vecs", [128, NV], F32)
    S32 = sb("S32", [128, 4, 128], F32)
    uh = sb("uh", [128, 4, 16], F32)
    stat = sb("stat", [128, 32], F32)
    lbt = sb("lbt", [128, 24], F32)
    Dd2 = sb("Dd2", [128, 2, 8], F32)
    itok2 = sb("itok2", [128, 2, 512], BF16)
    tot = sb("tot", [128, 4, 128], F32)
    psb = [es.enter_context(nc.psum_tensor(f"ps{i}", [128, 512], F32)) for i in range(8)]

    R_h = [Res(f"h{b}") for b in range(4)]
    R_xtok2 = [Res("xtok0"), Res("xtok1")]
    R_xnT = Res("xnT")
    R_ab = [Res(f"ab{c}") for c in range(8)]
    R_ar = [Res(f"ar{r}") for r in range(14)]
    R_ws = [Res(f"ws{i}") for i in range(3)]
    R_wsc = [Res(f"wsc{j}") for j in range(NJ)]
    R_prep = [Res(f"prep{j}") for j in range(4)]
    R_KP = [[Res(f"kp{s}_{g}") for g in range(4)] for s in range(5)]
    R_VP = [[Res(f"vp{s}_{g}") for g in range(4)] for s in range(5)]
    R_KN = [Res(f"kn{g}") for g in range(4)]
    R_VN = [Res(f"vn{g}") for g in range(4)]
    R_rope = Res("rope")
    R_ptok = R_ar[6]
    R_pT = R_ar[7]
    R_c = Res("consts", const=True)
    R_S = [Res(f"S{hh}") for hh in range(4)]
    R_uh = Res("uh")
    R_stat4 = [Res(f"stat{b}") for b in range(4)]
    R_lb = Res("lb", const=True)
    R_Dd2 = [Res("Dd0"), Res("Dd1")]
    R_itok2 = [Res("itok0"), Res("itok1")]
    R_tot = Res("tot")
    R_pb = [Res(f"pb{i}", excl=True) for i in range(8)]

    def cb(i):
        return cbf[:, i * 128:(i + 1) * 128]

    ident, ones_b, bd64, prot, hmask = cb(0), cb(1), cb(2), cb(3), cb(4)
    m1p, m1c, m16f, m16m, mpp, mpc = cb(5), cb(6), cb(7), cb(8), cb(9), cb(10)
    scanm = cf[:, 0:512]
    eps_ap = cf[:, 576:577]

    def vcol(i):
        return vec[:, i:i + 1]

    pstate = {"next": 0, "unread": [False] * 8}

    def pb():
        i = pstate["next"]
        pstate["next"] = (i + 1) % 6
        assert not pstate["unread"][i], f"psum bank {i} reallocated before read"
        pstate["unread"][i] = True
        return i

    def pread(i):
        pstate["unread"][i] = False

    def ps_bf(i):
        return psb[i][:].bitcast(BF16)

    R_c = Res("consts", const=True)
    P.op("pool", lambda e: e.dma_start(out=cbf[:], in_=cbf_d[:, :]), writes=[R_c], chan="constp")
    P.op("sp", lambda e: e.dma_start(out=cf[:], in_=cf_d[:, :]), reads=[R_c], writes=[R_c], chan="const")
    P.op("sp", lambda e: e.dma_start(out=vec[:], in_=vec_d[:, :]), reads=[R_c], writes=[R_c], chan="const")
    pw_sb = sb("pw_sb", [128, 4, 128], BF16)
    pool_w_d = dram("pool_w", [4, 128, 128], F32, "ExternalInput")
    P.op("pool", lambda e: e.dma_start(out=pw_sb[:], in_=pool_w_d.rearrange("g c d -> c g d")), reads=[R_c], writes=[R_c], chan="constp")
    P.op("dve", lambda e: e.memset(S32[:], 0.0), writes=R_S)
    P.op("dve", lambda e: e.memset(uh[:], 0.0), writes=[R_uh])
    P.op("act", lambda e: e.activation(lbt[:, 0:12], vec[:, 52:64], AF.Exp), reads=[R_c], writes=[R_lb])
    P.op("dve", lambda e: e.tensor_tensor(lbt[:, 20:24], lbt[:, 0:4], lbt[:, 4:8], ALU.add), reads=[R_lb], writes=[R_lb])
    P.op("dve", lambda e: e.tensor_tensor(lbt[:, 20:24], lbt[:, 20:24], lbt[:, 8:12], ALU.add), reads=[R_lb], writes=[R_lb])
    P.op("dve", lambda e: e.reciprocal(lbt[:, 20:24], lbt[:, 20:24]), reads=[R_lb], writes=[R_lb])
    P.op("dve", lambda e: e.tensor_tensor(lbt[:, 12:16], lbt[:, 0:4], lbt[:, 20:24], ALU.mult), reads=[R_lb], writes=[R_lb])
    P.op("dve", lambda e: e.tensor_scalar(lbt[:, 16:20], lbt[:, 12:16], -1.0, 1.0, ALU.mult, ALU.add), reads=[R_lb], writes=[R_lb])

    gain_of = {}
    for j in range(0, 5):
        gain_of[j] = 0
    for base, l in ((7, 0), (37, 1)):
        for qd in range(4):
            gain_of[base + 4 * qd] = 1 + 3 * l
            gain_of[base + 4 * qd + 1] = 1 + 3 * l
        gain_of[base + 17] = 2 + 3 * l
        gain_of[base + 19] = 2 + 3 * l
    for j in range(27, 35):
        gain_of[j] = 3
    nel = {j: SLOT for j in range(NJ)}
    for j in (23, 25, 53, 55):
        nel[j] = 1024
    for j in (28, 30, 32, 34):
        nel[j] = 2048
    stg_state = {"n": 0}
    R_thr = [Res(f"thr{j}") for j in range(NJ)]
    PREP_AHEAD = 6

    def emit_prep(j):
        n = nel[j]
        thr = [R_thr[j - PREP_AHEAD]] if j >= PREP_AHEAD else []
        if j in gain_of:
            stg_n = stg_state["n"]
            s = 1 + stg_n % 3
            use_v = (stg_n // 3) % 2 == 1
            stg_state["n"] += 1
            ncol = n // 8
            if use_v:
                stg = Vperm[:, 4 * s:4 * s + 4, :].rearrange("p a (b n) -> p (a b) n", b=2)[:, :, 0:ncol]
                stg_res = R_VP[s]
            else:
                stg = Kperm[:, :, s * T:s * T + ncol]
                stg_res = R_KP[s]
            tagc = ("v" if use_v else "k") + str(s)
            P.op("pool", lambda e, j=j, stg=stg, n=n: e.dma_start(out=stg, in_=wsrc[j, :, 0:n].rearrange("p (k n) -> p k n", k=8)),
                 reads=thr, writes=stg_res, chan=f"pw{tagc}")
            gi = gain_of[j]
            P.op("dve", lambda e, stg=stg, gi=gi, ncol=ncol: e.tensor_tensor(
                stg, stg, vec[:, gi * 8:gi * 8 + 8].unsqueeze(2).to_broadcast([128, 8, ncol]), ALU.mult),
                reads=[R_c] + stg_res, writes=stg_res, cost=2.4)
            P.op("sp", lambda e, j=j, stg=stg, n=n: e.dma_start(out=wsc[j, :, 0:n].rearrange("p (k n) -> p k n", k=8), in_=stg),
                 reads=stg_res, writes=[R_wsc[j]], chan=f"ps{tagc}")
        else:
            P.op("pool", lambda e, j=j, n=n: e.dma_start(out=wsc[j, :, 0:n], in_=wsrc[j, :, 0:n]),
                 reads=thr, writes=[R_wsc[j], R_prep[j % 4]], chan=f"prep{j % 4}")

    for j in range(min(PREP_AHEAD, NJ)):
        emit_prep(j)

    wst = {"loaded": -1}

    def wuse(i, k_first, k_last=None):
        if k_last is None:
            k_last = k_first
        n_first = i * NJ + k_first
        n_last = i * NJ + k_last
        assert n_last <= n_first + 2
        lim = min(n_first + 2, nt * NJ - 1)
        while wst["loaded"] < lim:
            n = wst["loaded"] + 1
            jt, s = n % NJ, n % 3
            ne = nel[jt]
            P.op("sp", lambda e, jt=jt, s=s, ne=ne: e.dma_start(out=wslot[s][:, 0:ne], in_=wsc[jt, :, 0:ne]),
                 reads=[R_wsc[jt]], writes=[R_ws[s]] + ([R_thr[n]] if n < NJ else []), chan=f"w{s}")
            wst["loaded"] = n
            if n + PREP_AHEAD < NJ:
                emit_prep(n + PREP_AHEAD)
        return [((i * NJ + k) % 3) for k in range(k_first, k_last + 1)]

    def w3(s, ncol=512):
        return wslot[s][:, 0:8 * ncol].rearrange("p (k n) -> p k n", k=8)

    def norm_stage(i):
        for b in range(4):
            xtok = xtok2[b % 2]
            R_xtok = R_xtok2[b % 2]
            R_stat = R_stat4[b]
            P.op("pool", lambda e, b=b: e.memset(stat[:, b:b + 1], 0.0), writes=[R_stat], cost=0.1)
            P.op("act", lambda e, b=b, xtok=xtok: e.activation(xtok[:], h[:, b, :], AF.Square, accum_out=stat[:, b:b + 1]),
                 reads=[R_h[b], R_stat], writes=[R_xtok, R_stat], cost=1.1)
            P.op("act", lambda e, b=b: e.activation(stat[:, 8 + b:9 + b], stat[:, b:b + 1], AF.Ln, bias=eps_ap, scale=1.0 / D),
                 reads=[R_stat, R_c], writes=[R_stat], cost=0.3)
            P.op("act", lambda e, b=b: e.activation(stat[:, 16 + b:17 + b], stat[:, 8 + b:9 + b], AF.Exp, scale=-0.5),
                 reads=[R_stat], writes=[R_stat], cost=0.3)
            P.op("dve", lambda e, b=b, xtok=xtok: e.tensor_scalar(xtok[:], h[:, b, :], stat[:, 16 + b:17 + b], None, ALU.mult),
                 reads=[R_h[b], R_stat, R_xtok], writes=[R_xtok], cost=1.2)
            k = pb()

            def tr(e, k=k, xtok=xtok):
                ins = None
                for c in range(8):
                    ins = e.transpose(ps_bf(k)[:, c * 128:(c + 1) * 128], xtok[:, c * 128:(c + 1) * 128], ident)
                return ins
            P.op("pe", tr, reads=[R_xtok, R_c], writes=[R_pb[k]], cost=1.0)
            P.op("dve", lambda e, k=k, b=b: e.tensor_copy(
                xnT[:, :, b * 128:(b + 1) * 128], ps_bf(k).rearrange("p (c t) -> p c t", c=8)),
                reads=[R_pb[k]], writes=[R_xnT], cost=1.05)
            pread(k)

    def tokmajor_proj(i, jk, src, src_res, nh, first_dst=None):
        (s,) = wuse(i, jk)
        for b in range(4):
            k = pb()

            def mm(e, k=k, b=b, s=s):
                ins = None
                for kc in range(8):
                    ins = e.matmul(psb[k][:, :], src[:, kc, b * 128:(b + 1) * 128], w3(s)[:, kc, :],
                                   start=(kc == 0), stop=(kc == 7))
                return ins
            P.op("pe", mm, reads=list(src_res) + [R_ws[s]], writes=[R_pb[k]])
            P.op("dve", lambda e, k=k, b=b, nh=nh: e.tensor_tensor(
                h[:, b, nh * 512:(nh + 1) * 512], h[:, b, nh * 512:(nh + 1) * 512], psb[k][:, :], ALU.add),
                reads=[R_pb[k], R_h[b]], writes=[R_h[b]])
            pread(k)

    def mlp_stage(i, base):
        norm_stage(i)
        hid = [arena[:, 0:4, :].bitcast(BF16), arena[:, 4:8, :].bitcast(BF16)]
        hid_res = [R_ar[0:4], R_ar[4:8]]

        def hv(par, c):
            return hid[par][:, c // 2, (c % 2) * 512:(c % 2) * 512 + 512]

        for qd in range(4):
            par = qd % 2
            for hf in range(2):
                (s,) = wuse(i, base + 4 * qd + hf)
                for m in range(4):
                    k = pb()
                    c = hf * 4 + m

                    def mm(e, k=k, m=m, s=s):
                        ins = None
                        for kc in range(8):
                            ins = e.matmul(psb[k][:, :], w3(s)[:, kc, m * 128:(m + 1) * 128], xnT[:, kc, :],
                                           start=(kc == 0), stop=(kc == 7))
                        return ins
                    P.op("pe", mm, reads=[R_xnT, R_ws[s]], writes=[R_pb[k]])
                    rr = hid_res[par][c // 2]
                    P.op("act", lambda e, k=k, par=par, c=c: e.activation(hv(par, c), psb[k][:, :], AF.Relu),
                         reads=[R_pb[k]], writes=[rr])
                    pread(k)
                    P.op("pool", lambda e, par=par, c=c: e.tensor_tensor(hv(par, c), hv(par, c), hv(par, c), ALU.mult),
                         reads=[rr], writes=[rr], cost=1.1)
            for nh in range(2):
                (s,) = wuse(i, base + 4 * qd + 2 + nh)
                for b in range(4):
                    k = pb()

                    def mm(e, k=k, b=b, s=s, par=par):
                        ins = None
                        for kc in range(8):
                            ins = e.matmul(psb[k][:, :], hv(par, kc)[:, b * 128:(b + 1) * 128], w3(s)[:, kc, :],
                                           start=(kc == 0), stop=(kc == 7))
                        return ins
                    P.op("pe", mm, reads=hid_res[par] + [R_ws[s]], writes=[R_pb[k]])
                    P.op("dve", lambda e, k=k, b=b, nh=nh: e.tensor_tensor(
                        h[:, b, nh * 512:(nh + 1) * 512], h[:, b, nh * 512:(nh + 1) * 512], psb[k][:, :], ALU.add),
                        reads=[R_pb[k], R_h[b]], writes=[R_h[b]])
                    pread(k)

    def ple_stage(i, base, l):
        norm_stage(i)
        P.op("pool", lambda e: e.dma_start(
            out=ptok, in_=p_d[l, i * T:(i + 1) * T, :].rearrange("(b p) f -> p b f", p=128)),
            writes=[R_ptok], chan="pld")
        k = pb()

        def tr(e, k=k):
            ins = None
            for pc in range(2):
                for b in range(4):
                    ins = e.transpose(ps_bf(k)[:, pc * 512 + b * 128:pc * 512 + (b + 1) * 128],
                                      ptok[:, b, pc * 128:(pc + 1) * 128], ident)
            return ins
        P.op("pe", tr, reads=[R_ptok, R_c], writes=[R_pb[k]])
        P.op("act", lambda e, k=k: e.activation(arena[:, 7, :].bitcast(BF16)[:, 0:1024], ps_bf(k), AF.Copy),
             reads=[R_pb[k]], writes=[R_pT])
        pread(k)
        for nh in range(2):
            sp_, sg = wuse(i, base + 2 * nh, base + 2 * nh + 1)
            for b in range(4):
                par = (nh * 4 + b) % 2
                gt = arena[:, 2 * par, 0:512]
                pw = arena[:, 2 * par + 1, 0:512]
                rg, rp = R_ar[2 * par], R_ar[2 * par + 1]
                kg = pb()

                def mmg(e, kg=kg, b=b, sg=sg):
                    ins = None
                    for kc in range(8):
                        ins = e.matmul(psb[kg][:, :], xnT[:, kc, b * 128:(b + 1) * 128], w3(sg)[:, kc, :],
                                       start=(kc == 0), stop=(kc == 7))
                    return ins
                P.op("pe", mmg, reads=[R_xnT, R_ws[sg]], writes=[R_pb[kg]])
                kp_ = pb()

                def mmp(e, kp_=kp_, b=b, sp_=sp_):
                    ins = None
                    wv = wslot[sp_][:, 0:1024].rearrange("p (k n) -> p k n", k=2)
                    for pc in range(2):
                        ins = e.matmul(psb[kp_][:, :], pT[:, pc, b * 128:(b + 1) * 128], wv[:, pc, :],
                                       start=(pc == 0), stop=(pc == 1))
                    return ins
                P.op("pe", mmp, reads=[R_pT, R_ws[sp_]], writes=[R_pb[kp_]])
                P.op("act", lambda e, kg=kg, gt=gt: e.activation(gt, psb[kg][:, :], AF.Sigmoid),
                     reads=[R_pb[kg]], writes=[rg])
                pread(kg)
                P.op("dve", lambda e, kp_=kp_, gt=gt, pw=pw: e.tensor_tensor(pw, psb[kp_][:, :], gt, ALU.mult),
                     reads=[R_pb[kp_], rg], writes=[rp])
                pread(kp_)
                P.op("dve", lambda e, b=b, nh=nh, pw=pw: e.tensor_tensor(
                    h[:, b, nh * 512:(nh + 1) * 512], h[:, b, nh * 512:(nh + 1) * 512], pw, ALU.add),
                    reads=[rp, R_h[b]], writes=[R_h[b]])

    def l0_mixer(i):
        norm_stage(i)
        if DBG == 1:
            return
        (s,) = wuse(i, 0)
        for g in range(4):
            ub = arena[:, 5, :]
            tA = arena[:, 6, :]
            tB = arena[:, 7, :]
            dd = arena[:, 9, :].bitcast(BF16)[:, 0:512]
            k = pb()

            def mm(e, k=k, g=g, s=s):
                ins = None
                for kc in range(8):
                    ins = e.matmul(psb[k][:, :], w3(s)[:, kc, g * 128:(g + 1) * 128], xnT[:, kc, :],
                                   start=(kc == 0), stop=(kc == 7))
                return ins
            P.op("pe", mm, reads=[R_xnT, R_ws[s]], writes=[R_pb[k]])
            P.op("dve", lambda e, g=g: e.tensor_copy(ub[:, 0:16], uh[:, g, :]), reads=[R_uh], writes=[R_ar[5]])
            P.op("act", lambda e, k=k: e.activation(ub[:, 16:528], psb[k][:, :], AF.Copy),
                 reads=[R_pb[k]], writes=[R_ar[5]])
            pread(k)
            P.op("dve", lambda e, g=g: e.tensor_copy(uh[:, g, :], ub[:, 512:528]), reads=[R_ar[5]], writes=[R_uh])
            src, src_r = ub, R_ar[5]
            bufs = [(tA, R_ar[6]), (tB, R_ar[7])]
            sh = 1
            for step in range(g + 1):
                dst, dst_r = bufs[step % 2]
                lo = 2 * sh - 1
                P.op("dve", lambda e, src=src, dst=dst, sh=sh, lo=lo: e.tensor_tensor(
                    dst[:, lo:528], src[:, lo:528], src[:, lo - sh:528 - sh], ALU.add),
                    reads=[src_r], writes=[dst_r])
                src, src_r = dst, dst_r
                sh *= 2
            w = 2 ** (g + 1)
            P.op("dve", lambda e, src=src, w=w: e.scalar_tensor_tensor(
                dd, src[:, 16:528], 1.0 / w, ub[:, 16:528], ALU.mult, ALU.subtract),
                reads=[src_r, R_ar[5]], writes=[R_ar[9]])
            if i == 0:
                P.op("dve", lambda e, src=src, g=g: e.tensor_tensor(
                    src[:, 16:32], src[:, 16:32], cf[:, 512 + g * 16:512 + (g + 1) * 16], ALU.mult),
                    reads=[src_r, R_c], writes=[src_r])
                P.op("dve", lambda e, src=src: e.tensor_tensor(dd[:, 0:16], src[:, 16:32], ub[:, 16:32], ALU.subtract),
                     reads=[src_r, R_ar[5], R_ar[9]], writes=[R_ar[9]])
            k2 = pb()
            P.op("pe", lambda e, k2=k2, g=g: e.matmul(psb[k2][:, :], pw_sb[:, g, :], dd, start=True, stop=True),
                 reads=[R_ar[9], R_c], writes=[R_pb[k2]])
            P.op("act", lambda e, k2=k2, g=g: e.activation(abT[:, g, :], psb[k2][:, :], AF.Copy, scale=vcol(48 + g)),
                 reads=[R_pb[k2], R_c], writes=[R_ab[g]])
            pread(k2)
        if DBG == 2:
            return
        A0 = arena[:, 0, 0:512]
        A1 = arena[:, 1, 0:512]
        A2 = arena[:, 2, 0:512]
        A3 = arena[:, 3, 0:512]
        A4s = [(arena[:, 8, 0:512], R_ar[8]), (arena[:, 10, 0:512], R_ar[10])]
        QKs = [(arena[:, 4, :].bitcast(BF16), R_ar[4]), (arena[:, 11, :].bitcast(BF16), R_ar[11])]
        r5 = arena[:, 5, :].bitcast(BF16)
        r6 = arena[:, 6, :].bitcast(BF16)
        r7 = arena[:, 7, :].bitcast(BF16)
        T9 = arena[:, 9, 0:512]
        khT = r5[:, 512:1024].rearrange("p (b v) -> p b v", b=4)
        AT = r6[:, 0:512].rearrange("p (b v) -> p b v", b=4)
        osq = r6[:, 512:1024]
        Sdb = r7[:, 0:1024].rearrange("p (c v) -> p c v", c=8)

        def early(hh):
            par = hh % 2
            A4, rA4 = A4s[par]
            QK, rQK = QKs[par]
            qt, kh = QK[:, 0:512], QK[:, 512:1024]
            Ddp = Dd2[:, par, :]
            rDd = R_Dd2[par]
            (s,) = wuse(i, 1 + hh)
            kq, kf, kg = pb(), pb(), pb()
            for kk, off in ((kq, 0), (kf, 128), (kg, 256)):
                def mm(e, kk=kk, off=off, s=s):
                    ins = None
                    for kc in range(8):
                        ins = e.matmul(psb[kk][:, :], w3(s)[:, kc, off:off + 128], xnT[:, kc, :],
                                       start=(kc == 0), stop=(kc == 7))
                    return ins
                P.op("pe", mm, reads=[R_xnT, R_ws[s]], writes=[R_pb[kk]])
            ki = pb()

            def mmi(e, ki=ki, s=s):
                ins = None
                for b in range(4):
                    for kc in range(8):
                        ins = e.matmul(psb[ki][:, b * 128:(b + 1) * 128], xnT[:, kc, b * 128:(b + 1) * 128],
                                       w3(s)[:, kc, 384:512], start=(kc == 0), stop=(kc == 7))
                return ins
            P.op("pe", mmi, reads=[R_xnT, R_ws[s]], writes=[R_pb[ki]])
            P.op("act", lambda e, kf=kf: e.activation(A0, psb[kf][:, :], AF.Sigmoid),
                 reads=[R_pb[kf]], writes=[R_ar[0]])
            pread(kf)
            P.op("act", lambda e, kg=kg, A4=A4: e.activation(A4, psb[kg][:, :], AF.Sigmoid),
                 reads=[R_pb[kg]], writes=[rA4])
            P.op("act", lambda e, ki=ki, par=par: e.activation(itok2[:, par, :], psb[ki][:, :], AF.Copy),
                 reads=[R_pb[ki]], writes=[R_itok2[par]])
            pread(ki)
            P.op("dve", lambda e, kg=kg, A4=A4: e.scalar_tensor_tensor(A4, psb[kg][:, :], vcol(64), A4, ALU.mult, ALU.mult),
                 reads=[R_pb[kg], rA4, R_c], writes=[rA4])
            pread(kg)
            P.op("dve", lambda e, hh=hh: e.tensor_scalar(A0, A0, lbt[:, 16 + hh:17 + hh], lbt[:, 12 + hh:13 + hh],
                                                        ALU.mult, ALU.add),
                 reads=[R_ar[0], R_lb], writes=[R_ar[0]])
            P.op("act", lambda e: e.activation(A1, A0, AF.Ln), reads=[R_ar[0]], writes=[R_ar[1]])
            P.op("dve", lambda e: e.tensor_scalar(A0, A0, -1.0, 1.0, ALU.mult, ALU.add),
                 reads=[R_ar[0], R_ar[1]], writes=[R_ar[0]])
            P.op("dve", lambda e: e.tensor_tensor_scan(A2, scanm, A1, 0.0, ALU.mult, ALU.add),
                 reads=[R_ar[1], R_c], writes=[R_ar[2]])
            A1v = A1.rearrange("p (c j) -> p c j", j=64)
            A2v = A2.rearrange("p (c j) -> p c j", j=64)
            P.op("dve", lambda e: e.tensor_tensor(A1v, A2v, A2v[:, :, 63:64].to_broadcast([128, 8, 64]), ALU.subtract),
                 reads=[R_ar[2], R_ar[1]], writes=[R_ar[1]])
            P.op("act", lambda e, Ddp=Ddp: e.activation(Ddp, A2[:, 63:512:64], AF.Exp), reads=[R_ar[2]], writes=[rDd])
            P.op("act", lambda e: e.activation(A3, A1, AF.Exp, scale=-1.0), reads=[R_ar[1]], writes=[R_ar[3]])
            P.op("act", lambda e: e.activation(A1, A1, AF.Exp), reads=[R_ar[1], R_ar[3]], writes=[R_ar[1]])
            P.op("dve", lambda e, kq=kq, qt=qt: e.tensor_tensor(qt, psb[kq][:, :], A1, ALU.mult),
                 reads=[R_pb[kq], R_ar[1]], writes=[rQK])
            pread(kq)
            P.op("dve", lambda e, kh=kh: e.tensor_tensor(kh, A0, A3, ALU.mult),
                 reads=[R_ar[0], R_ar[3], rQK], writes=[rQK])

        def late(hh):
            par = hh % 2
            A4, rA4 = A4s[par]
            QK, rQK = QKs[par]
            qt, kh = QK[:, 0:512], QK[:, 512:1024]
            Ddp = Dd2[:, par, :]
            rDd = R_Dd2[par]
            itokh = itok2[:, par, :].rearrange("p (b v) -> p b v", b=4)
            rIt = R_itok2[par]
            kt = pb()

            def trk(e, kt=kt, kh=kh):
                ins = None
                for b in range(4):
                    ins = e.transpose(ps_bf(kt)[:, b * 128:(b + 1) * 128], kh[:, b * 128:(b + 1) * 128], ident)
                return ins
            P.op("pe", trk, reads=[rQK, R_c], writes=[R_pb[kt]], cost=0.5)
            P.op("act", lambda e, kt=kt: e.activation(r5[:, 512:1024], ps_bf(kt)[:, 0:512], AF.Copy),
                 reads=[R_pb[kt]], writes=[R_ar[5]])
            pread(kt)
            ks = pb()

            def sc(e, ks=ks, kh=kh, qt=qt):
                ins = None
                for b in range(4):
                    ins = e.matmul(psb[ks][:, b * 128:(b + 1) * 128], kh[:, b * 128:(b + 1) * 128],
                                   qt[:, b * 128:(b + 1) * 128], start=True, stop=True)
                return ins
            P.op("pe", sc, reads=[rQK], writes=[R_pb[ks]], cost=0.5)
            P.op("dve", lambda e, ks=ks: e.tensor_tensor(
                AT, psb[ks][:, :].rearrange("p (b v) -> p b v", b=4),
                hmask.unsqueeze(1).to_broadcast([128, 4, 128]), ALU.mult),
                reads=[R_pb[ks], R_c], writes=[R_ar[6]])
            pread(ks)
            ku = [pb(), pb()]

            def um(e, ku=ku, itokh=itokh):
                ins = None
                for c in range(8):
                    b, pr_ = c // 2, c % 2
                    ins = e.matmul(psb[ku[pr_]][:, b * 128:(b + 1) * 128],
                                   khT[64 * pr_:64 * pr_ + 64, b, :], itokh[64 * pr_:64 * pr_ + 64, b, :],
                                   start=True, stop=True)
                return ins
            P.op("pe", um, reads=[R_ar[5], rIt], writes=[R_pb[ku[0]], R_pb[ku[1]]], cost=0.6)
            for c in range(8):
                P.op("act", lambda e, c=c, hh=hh, Ddp=Ddp: e.activation(Sdb[:, c, :], S32[:, hh, :], AF.Copy, scale=Ddp[:, c:c + 1]),
                     reads=[R_S[hh], rDd], writes=[R_ar[7]], cost=0.3)
                P.op("dve", lambda e, c=c, hh=hh, ku=ku, Ddp=Ddp: e.scalar_tensor_tensor(
                    S32[:, hh, :], S32[:, hh, :], Ddp[:, c:c + 1], psb[ku[c % 2]][:, (c // 2) * 128:(c // 2 + 1) * 128],
                    ALU.mult, ALU.add),
                    reads=[R_S[hh], rDd, R_pb[ku[c % 2]]], writes=[R_S[hh]], cost=0.35)
            pread(ku[0])
            pread(ku[1])
            ko = pb()

            def om(e, ko=ko, qt=qt, itokh=itokh):
                ins = None
                first = True
                for b in range(4):
                    for c in (2 * b, 2 * b + 1):
                        ins = e.matmul(psb[ko][:, c * 64:(c + 1) * 64], Sdb[:, c, :], qt[:, c * 64:(c + 1) * 64],
                                       start=first, stop=False)
                        first = False
                    ins = e.matmul(psb[ko][:, b * 128:(b + 1) * 128], itokh[:, b, :], AT[:, b, :],
                                   start=False, stop=(b == 3))
                return ins
            P.op("pe", om, reads=[R_ar[7], rQK, rIt, R_ar[6]], writes=[R_pb[ko]], cost=0.9)
            P.op("act", lambda e, ko=ko: e.activation(osq, psb[ko][:, :], AF.Square),
                 reads=[R_pb[ko], R_ar[6]], writes=[R_ar[6]])
            kn = pb()
            P.op("pe", lambda e, kn=kn: e.matmul(psb[kn][:, :], ones_b, osq, start=True, stop=True),
                 reads=[R_ar[6], R_c], writes=[R_pb[kn]])
            P.op("act", lambda e, kn=kn: e.activation(T9, psb[kn][:, :], AF.Ln, bias=eps_ap, scale=1.0 / 128),
                 reads=[R_pb[kn], R_c], writes=[R_ar[9]])
            pread(kn)
            P.op("act", lambda e: e.activation(T9, T9, AF.Exp, scale=-0.5), reads=[R_ar[9]], writes=[R_ar[9]])
            P.op("dve", lambda e, ko=ko: e.tensor_tensor(T9, psb[ko][:, :], T9, ALU.mult),
                 reads=[R_pb[ko], R_ar[9]], writes=[R_ar[9]])
            pread(ko)
            P.op("dve", lambda e, hh=hh, A4=A4: e.tensor_tensor(abT[:, 4 + hh, :], T9, A4, ALU.mult),
                 reads=[R_ar[9], rA4], writes=[R_ab[4 + hh]])

        early(0)
        for hh in range(4):
            if hh + 1 < 4:
                early(hh + 1)
            late(hh)
        if DBG == 3:
            return
        tokmajor_proj(i, 5, abT, R_ab, 0)
        tokmajor_proj(i, 6, abT, R_ab, 1)

    def l1_mixer(i):
        norm_stage(i)
        sl = i % 5
        P.op("pool", lambda e: e.dma_start(out=cosb[:], in_=cos_d[:, i * T:(i + 1) * T]), writes=[R_rope], chan="rope")
        P.op("pool", lambda e: e.dma_start(out=sinb[:], in_=sin_d[:, i * T:(i + 1) * T]), writes=[R_rope], chan="rope")
        X = arena[:, 0, :].bitcast(BF16)
        zb, sq = X[:, 0:512], X[:, 512:1024]
        Y = arena[:, 1, 0:512]
        Z = arena[:, 2, 0:512]
        W = arena[:, 3, 0:512]
        natacc = arena[:, 0:4, 0:512]
        QNr, QPr = (4, 12), (5, 13)

        def qpad(regs, j, hp):
            return arena[:, regs[j], :].bitcast(BF16)[:, hp * 512:(hp + 1) * 512]
        for r_ in (4, 5, 12, 13):
            P.op("pool", lambda e, r_=r_: e.memset(arena[:, r_, :], 0.0), writes=[R_ar[r_]], cost=1.0)
        PT = [arena[:, 6, :].bitcast(BF16)[:, 0:512], arena[:, 6, :].bitcast(BF16)[:, 512:1024],
              arena[:, 7, :].bitcast(BF16)[:, 0:512], arena[:, 7, :].bitcast(BF16)[:, 512:1024]]
        PT_res = [R_ar[6], R_ar[6], R_ar[7], R_ar[7]]
        XS = [(arena[:, 0, :].bitcast(BF16), arena[:, 1, 0:512], arena[:, 2, 0:512], arena[:, 3, 0:512], R_ar[0], R_ar[1], R_ar[2], R_ar[3]),
              (arena[:, 8, :].bitcast(BF16), arena[:, 9, 0:512], arena[:, 10, 0:512], arena[:, 11, 0:512], R_ar[8], R_ar[9], R_ar[10], R_ar[11])]
        for g in range(4):
            (s,) = wuse(i, 27 + 2 * g)
            st1 = {}

            def stage1(wi, g=g, s=s):
                Xb, Y_, Z_, W_, rX, rY, rZ, rW = XS[wi % 2]
                zb_, sq_ = Xb[:, 0:512], Xb[:, 512:1024]
                col0 = wi * 128
                kz = pb()

                def mm(e, kz=kz, col0=col0, s=s):
                    ins = None
                    for kc in range(8):
                        ins = e.matmul(psb[kz][:, :], w3(s)[:, kc, col0:col0 + 128], xnT[:, kc, :],
                                       start=(kc == 0), stop=(kc == 7))
                    return ins
                P.op("pe", mm, reads=[R_xnT, R_ws[s]], writes=[R_pb[kz]])
                P.op("act", lambda e, kz=kz, zb_=zb_: e.activation(zb_, psb[kz][:, :], AF.Copy), reads=[R_pb[kz]], writes=[rX])
                P.op("act", lambda e, kz=kz, sq_=sq_: e.activation(sq_, psb[kz][:, :], AF.Square), reads=[R_pb[kz]], writes=[rX])
                st1[wi] = kz

            def stage2(wi, g=g):
                Xb, Y_, Z_, W_, rX, rY, rZ, rW = XS[wi % 2]
                zb_, sq_ = Xb[:, 0:512], Xb[:, 512:1024]
                isk = wi >= 2
                j = wi % 2
                kz = st1[wi]
                kr, kss = pb(), pb()
                P.op("pe", lambda e, kr=kr, zb_=zb_: e.matmul(psb[kr][:, :], prot, zb_, start=True, stop=True),
                     reads=[rX, R_c], writes=[R_pb[kr]], cost=0.3)
                P.op("pe", lambda e, kss=kss, sq_=sq_: e.matmul(psb[kss][:, :], bd64, sq_, start=True, stop=True),
                     reads=[rX, R_c], writes=[R_pb[kss]], cost=0.3)
                gc = 67 if isk else 65
                P.op("dve", lambda e, kz=kz, gc=gc, Y_=Y_: e.scalar_tensor_tensor(Y_, psb[kz][:, :], vcol(gc), cosb[:], ALU.mult, ALU.mult),
                     reads=[R_pb[kz], R_rope, R_c], writes=[rY])
                pread(kz)
                P.op("dve", lambda e, kr=kr, gc=gc, Z_=Z_: e.scalar_tensor_tensor(Z_, psb[kr][:, :], vcol(gc + 1), sinb[:], ALU.mult, ALU.mult),
                     reads=[R_pb[kr], R_rope, R_c], writes=[rZ])
                pread(kr)
                P.op("act", lambda e, kss=kss, W_=W_: e.activation(W_, psb[kss][:, :], AF.Ln, bias=eps_ap, scale=1.0 / 64),
                     reads=[R_pb[kss], R_c], writes=[rW])
                pread(kss)
                P.op("act", lambda e, W_=W_: e.activation(W_, W_, AF.Exp, scale=-0.5), reads=[rW], writes=[rW])
                P.op("dve", lambda e, Y_=Y_, Z_=Z_: e.tensor_tensor(Y_, Y_, Z_, ALU.add), reads=[rY, rZ], writes=[rY])
                if not isk:
                    for hp in range(2):
                        rows = slice(64 * hp, 64 * hp + 64)
                        dn, dn_r = qpad(QNr, j, hp), R_ar[QNr[j]]
                        dp, dp_r = qpad(QPr, j, hp), R_ar[QPr[j]]
                        P.op("dve", lambda e, dn=dn, Y_=Y_, W_=W_, rows=rows: e.tensor_tensor(dn[rows, :], Y_[rows, :], W_[rows, :], ALU.mult),
                             reads=[rY, rW], writes=[dn_r])
                        P.op("pool", lambda e, dn=dn, dp=dp, rows=rows: e.tensor_copy(
                            dp[rows, :].rearrange("p (r j) -> p r j", r=4), dn[rows, :].rearrange("p (j r) -> p r j", r=4)),
                            reads=[dn_r], writes=[dp_r], cost=1.6)
                else:
                    dn, dn_r = Knat[:, 2 * g + j, 128:128 + T], R_KN[g]
                    dp, dp_r = Kperm[:, 2 * g + j, sl * T:(sl + 1) * T], R_KP[sl][g]
                    P.op("dve", lambda e, dn=dn, Y_=Y_, W_=W_: e.tensor_tensor(dn, Y_, W_, ALU.mult),
                         reads=[rY, rW], writes=[dn_r])
                    P.op("pool", lambda e, dn=dn, dp=dp: e.tensor_copy(
                        dp.rearrange("p (r j) -> p r j", r=4), dn.rearrange("p (j r) -> p r j", r=4)),
                        reads=[dn_r], writes=[dp_r], cost=1.6)

            stage1(0)
            for wi in range(4):
                if wi + 1 < 4:
                    stage1(wi + 1)
                stage2(wi)
            if DBG in (20, 30, 31, 32, 33, 34, 35):
                for _k in range(8):
                    pread(_k)
                continue
            (sv,) = wuse(i, 28 + 2 * g)
            wv = wslot[sv][:, 0:2048].rearrange("p (k n) -> p k n", k=8)
            for half in range(2):
                for perm in (False, True):
                    kv = pb()

                    def mmv(e, kv=kv, half=half, perm=perm, wv=wv):
                        ins = None
                        for bb in range(2):
                            b = half * 2 + bb
                            for kc in range(8):
                                lhs = xnT[:, kc, b:T:4] if perm else xnT[:, kc, b * 128:(b + 1) * 128]
                                ins = e.matmul(psb[kv][:, bb * 256:(bb + 1) * 256], lhs, wv[:, kc, :],
                                               start=(kc == 0), stop=(kc == 7))
                        return ins
                    P.op("pe", mmv, reads=[R_xnT, R_ws[sv]], writes=[R_pb[kv]])
                    if perm:
                        dst = Vperm[:, sl * 4 + half * 2:sl * 4 + half * 2 + 2, 256 * g:256 * g + 256]
                        dr = R_VP[sl][g]
                    else:
                        dst = Vnat[:, 1 + half * 2:3 + half * 2, 256 * g:256 * g + 256]
                        dr = R_VN[g]
                    P.op("act", lambda e, kv=kv, dst=dst: e.activation(
                        dst, psb[kv][:, :].rearrange("p (b n) -> p b n", b=2), AF.Copy),
                        reads=[R_pb[kv]], writes=[dr])
                    pread(kv)
            if DBG == 21:
                for _k in range(8):
                    pread(_k)
                continue
            pairs = []
            acci = 0
            for qb in range(4):
                lst = []
                if not (i == 0 and qb == 0):
                    lst.append((lambda j, qb=qb, g=g: Knat[:, 2 * g + j, qb * 128:(qb + 1) * 128], Vnat[:, qb, :], m1p,
                                [R_KN[g], R_VN[g]]))
                lst.append((lambda j, qb=qb, g=g: Knat[:, 2 * g + j, (qb + 1) * 128:(qb + 2) * 128], Vnat[:, qb + 1, :], m1c,
                            [R_KN[g], R_VN[g]]))
                for n, (kf_, v_, m_, rr_) in enumerate(lst):
                    pairs.append(dict(q=(QNr, qb, [R_ar[4], R_ar[12]]), kf=kf_, v=v_, m=m_, rr=rr_, acc=6 + acci % 2,
                                      first=(n == 0), last=(n == len(lst) - 1), nat=True, idx=qb))
                acci += 1
            for c in range(4):
                lst = []
                for mback, msk in ((4, m16f), (3, m16m), (2, m16m), (1, mpp), (0, mpc)):
                    if i - mback < 0:
                        continue
                    s2 = (i - mback) % 5
                    lst.append((lambda j, s2=s2, c=c, g=g: Kperm[:, 2 * g + j, s2 * T + c * 128:s2 * T + (c + 1) * 128],
                                Vperm[:, s2 * 4 + c, :], msk, [R_KP[s2][g], R_VP[s2][g]]))
                for n, (kf_, v_, m_, rr_) in enumerate(lst):
                    pairs.append(dict(q=(QPr, c, [R_ar[5], R_ar[13]]), kf=kf_, v=v_, m=m_, rr=rr_, acc=6 + acci % 2,
                                      first=(n == 0), last=(n == len(lst) - 1), nat=False, idx=c))
                acci += 1

            def emit_scores(pr, n):
                kx = pb()
                qregs, qidx, qres = pr["q"]

                def scm(e, kx=kx, pr=pr, qregs=qregs, qidx=qidx):
                    ins = None
                    for hd in range(4):
                        j, hp = hd // 2, hd % 2
                        ins = e.matmul(psb[kx][:, hd * 128:(hd + 1) * 128],
                                       pr["kf"](j), qpad(qregs, j, hp)[:, qidx * 128:(qidx + 1) * 128],
                                       start=True, stop=True)
                    return ins
                P.op("pe", scm, reads=qres + pr["rr"], writes=[R_pb[kx]], cost=0.35)
                pr["ksc"] = kx
                pr["pt"] = n % 4

            def emit_softmax(pr):
                kx, ptb_, pres = pr["ksc"], PT[pr["pt"]], PT_res[pr["pt"]]
                P.op("act", lambda e, kx=kx, ptb_=ptb_: e.activation(ptb_, psb[kx][:, :], AF.Exp, scale=0.125),
                     reads=[R_pb[kx]], writes=[pres], cost=0.65)
                pread(kx)
                P.op("dve", lambda e, ptb_=ptb_, pr=pr: e.tensor_tensor(
                    ptb_.rearrange("p (h q) -> p h q", h=4), ptb_.rearrange("p (h q) -> p h q", h=4),
                    pr["m"].unsqueeze(1).to_broadcast([128, 4, 128]), ALU.mult),
                    reads=[pres, R_c], writes=[pres], cost=0.45)

            def emit_pv(pr):
                ptb_, pres, ka = PT[pr["pt"]], PT_res[pr["pt"]], pr["acc"]

                def pvm(e, ptb_=ptb_, pr=pr, ka=ka, g=g):
                    ins = None
                    for hd in range(4):
                        j, hp = hd // 2, hd % 2
                        rows = slice(64 * hp, 64 * hp + 64)
                        ins = e.matmul(psb[ka][rows, j * 128:(j + 1) * 128],
                                       pr["v"][:, 256 * g + 64 * hd:256 * g + 64 * hd + 64],
                                       ptb_[:, hd * 128:(hd + 1) * 128],
                                       start=(pr["first"] and j == 0), stop=False)
                        ins = e.matmul(psb[ka][rows, (2 + j) * 128:(3 + j) * 128],
                                       ones_b[:, 0:64], ptb_[:, hd * 128:(hd + 1) * 128],
                                       start=False, stop=(pr["last"] and j == 1))
                    return ins
                P.op("pe", pvm, reads=[pres, R_c] + pr["rr"], writes=[R_pb[ka]], cost=0.7)
                if pr["last"]:
                    if pr["nat"]:
                        qb = pr["idx"]
                        P.op("act", lambda e, ka=ka, qb=qb: e.activation(
                            natacc[:, :, qb * 128:(qb + 1) * 128], psb[ka][:, :].rearrange("p (k q) -> p k q", k=4), AF.Copy),
                            reads=[R_pb[ka]], writes=R_ar[0:4])
                    else:
                        c = pr["idx"]
                        P.op("dve", lambda e, ka=ka, c=c: e.tensor_tensor(
                            tot[:], psb[ka][:, :].rearrange("p (k q) -> p k q", k=4), natacc[:, :, c:T:4], ALU.add),
                            reads=[R_pb[ka]] + R_ar[0:4], writes=[R_tot])
                        P.op("dve", lambda e: e.reciprocal(tot[:, 2:4, :], tot[:, 2:4, :]), reads=[R_tot], writes=[R_tot])
                        P.op("dve", lambda e, c=c, g=g: e.tensor_tensor(
                            abT[:, 2 * g:2 * g + 2, c:T:4], tot[:, 0:2, :], tot[:, 2:4, :], ALU.mult),
                            reads=[R_tot], writes=[R_ab[2 * g], R_ab[2 * g + 1]])

            for n, pr in enumerate(pairs):
                emit_scores(pr, n)
                emit_softmax(pr)
                if n >= 3:
                    emit_pv(pairs[n - 3])
            for pr_ in pairs[-3:]:
                emit_pv(pr_)
            P.op("pool", lambda e, g=g: e.tensor_copy(Knat[:, 2 * g:2 * g + 2, 0:128], Knat[:, 2 * g:2 * g + 2, T:T + 128]),
                 reads=[R_KN[g]], writes=[R_KN[g]])
            P.op("pool", lambda e, g=g: e.tensor_copy(Vnat[:, 0, 256 * g:256 * g + 256], Vnat[:, 4, 256 * g:256 * g + 256]),
                 reads=[R_VN[g]], writes=[R_VN[g]])
        if DBG == 40:
            P.op("dve", lambda e: e.tensor_copy(h[:].rearrange("p b d -> p (b d)"), abT[:].rearrange("p c t -> p (c t)")),
                 reads=R_ab, writes=R_h)
            return
        if DBG == 41:
            P.op("dve", lambda e: e.tensor_copy(h[:].rearrange("p b d -> p (b d)"), Knat[:, :, 128:640].rearrange("p c t -> p (c t)")),
                 reads=R_KN, writes=R_h)
            return
        if DBG == 42:
            P.op("dve", lambda e: e.tensor_copy(h[:].rearrange("p b d -> p (b d)"), Vnat[:, 1:5, :].rearrange("p c t -> p (c t)")),
                 reads=R_VN, writes=R_h)
            return
        tokmajor_proj(i, 35, abT, R_ab, 0)
        tokmajor_proj(i, 36, abT, R_ab, 1)


    for i in range(nt):
        for b in range(4):
            P.op("sp", lambda e, b=b, i=i: e.dma_start(out=h[:, b, :], in_=x_d[i * T + b * 128:i * T + (b + 1) * 128, :]),
                 writes=[R_h[b]], chan=f"xin{b}")
        if upto >= 1:
            P.tag = f"{i}:l0mix"
            l0_mixer(i)
        if upto >= 2:
            P.tag = f"{i}:mlp0"
            mlp_stage(i, 7)
        if upto >= 3:
            P.tag = f"{i}:ple0"
            ple_stage(i, 23, 0)
        if upto >= 4:
            P.tag = f"{i}:l1mix"
            l1_mixer(i)
        if upto >= 5:
            P.tag = f"{i}:mlp1"
            mlp_stage(i, 37)
        if upto >= 6:
            P.tag = f"{i}:ple1"
            ple_stage(i, 53, 1)
        P.tag = f"{i}:io"
        for b in range(4):
            P.op("sp", lambda e, b=b, i=i: e.dma_start(out=out_d[i * T + b * 128:i * T + (b + 1) * 128, :], in_=h[:, b, :]),
                 reads=[R_h[b]], chan=f"out{b}")

    sems = {k: es.enter_context(nc.semaphore("s_" + k)) for k in ("pe", "act", "dve", "pool")}
    if SCHED:
        P.schedule()
        print('sched sim_time us', P.sim_time, 'nops', len(P.ops), flush=True)
    else:
        P.by_eng = {e: [x for x in P.ops if x.eng == e] for e in ENGS}
    chan_names = sorted({x.chan for x in P.ops if x.chan is not None})
    chs = {k: es.enter_context(nc.semaphore("c_" + k)) for k in chan_names}
    with nc.Block() as block:
        P.emit(block, sems, chs, final_waits=[f"out{b}" for b in range(4)])
    es.close()
    return nc


def run(inputs, nt=SEQ // T, upto=6, n_cores=8, trace=False):
    S = nt * T
    ws = _pack_weights(inputs)
    vecs = _pack_vec(inputs)
    cbf, cf, cosd, sind = _consts(S)
    pool_w = np.ascontiguousarray(inputs["pool_w"][0])
    nc = build(nt, upto)
    x, p = inputs["x"], inputs["p"]
    in_maps = []
    for c in range(n_cores):
        in_maps.append({
            "x": np.ascontiguousarray(x[c, :S]), "p": np.ascontiguousarray(p[:, c, :S]),
            "wsrc": ws, "vec": vecs, "cbf": cbf, "cf": cf, "cosd": cosd, "sind": sind, "pool_w": pool_w,
        })
    res = run_bass_kernel_spmd(nc, in_maps, core_ids=list(range(n_cores)), trace=trace)
    out = np.stack([res.results[c]["out"] for c in range(n_cores)], axis=0)
    return out, res


def kernel(**inputs):
    inputs = {k: np.asarray(v) for k, v in inputs.items()}
    out, _ = run(inputs)
    return out.astype(np.float32)
```
